# Optimizing a Trainium2 kernel written in Bass

```python
import math
import jax, jax.numpy as jnp
from jax import lax
import numpy as np

D_MODEL = 2048
BATCH = 4
SEQ = 2048
DEPTH = 2

CHUNK = 64
N_META = 16
D_CONV = D_MODEL
CONV_A_WIDTH = 31
SSM_EXPAND = 2
D_INNER = SSM_EXPAND * D_MODEL
HEAD_DIM = 64
N_SSM_HEADS = D_INNER // HEAD_DIM
N_GROUPS = 8
D_STATE = 128
CONV_B_WIDTH = 4
D_XBC = D_INNER + 2 * N_GROUPS * D_STATE
D_FF = 5632
CONV_F_WIDTH = 3
P_IN = 2 * D_CONV + D_INNER + D_XBC + N_SSM_HEADS + 2 * D_MODEL
ALPHA = (2.0 * DEPTH) ** 0.25
BETA = (8.0 * DEPTH) ** -0.25
EPS = 1e-5

kernel_name = "hybrid_conformer_ssd_gated_deepnorm"


def layer_norm(x, g, b):
    xf = x.astype(jnp.float32)
    mu = jnp.mean(xf, axis=-1, keepdims=True)
    var = jnp.mean(jnp.square(xf - mu), axis=-1, keepdims=True)
    y = (xf - mu) * lax.rsqrt(var + EPS) * g.astype(jnp.float32) + b.astype(jnp.float32)
    return y.astype(x.dtype)


def causal_dwconv(x, w, b):
    K, C = w.shape
    out = lax.conv_general_dilated(
        x, w[:, None, :].astype(x.dtype), window_strides=(1,), padding=[(K - 1, 0)],
        dimension_numbers=("NWC", "WIO", "NWC"), feature_group_count=C)
    return out + b.astype(x.dtype)


def segsum_exp(a):
    q = a.shape[-1]
    cs = jnp.cumsum(a, axis=-1)
    diff = cs[..., :, None] - cs[..., None, :]
    mask = jnp.tril(jnp.ones((q, q), dtype=bool))
    return jnp.where(mask, jnp.exp(jnp.where(mask, diff, 0.0)), 0.0)


def ssd_chunked(xs, dt, a, bs, cs):
    b, L, H, P = xs.shape
    pad = (-L) % CHUNK
    R = H // N_GROUPS
    pw = ((0, 0), (pad, 0), (0, 0), (0, 0))
    xdt = jnp.pad(xs * dt[..., None].astype(xs.dtype), pw)
    da = jnp.pad(dt * a, ((0, 0), (pad, 0), (0, 0)))
    bs = jnp.pad(bs, pw)
    cs = jnp.pad(cs, pw)
    nc = (L + pad) // CHUNK
    xdt = xdt.reshape(b, nc, CHUNK, N_GROUPS, R, P)
    bs = bs.reshape(b, nc, CHUNK, N_GROUPS, D_STATE)
    cs = cs.reshape(b, nc, CHUNK, N_GROUPS, D_STATE)
    da = da.reshape(b, nc, CHUNK, N_GROUPS, R).transpose(0, 3, 4, 1, 2)
    da_cum = jnp.cumsum(da, axis=-1)

    decay = segsum_exp(da).astype(xs.dtype)
    cb = jnp.einsum("bcqgn,bckgn->bgcqk", cs, bs)
    m = cb[:, :, None] * decay
    y_diag = jnp.einsum("bgrcqk,bckgrp->bcqgrp", m, xdt)

    decay_to_end = jnp.exp(da_cum[..., -1:] - da_cum).transpose(0, 3, 4, 1, 2).astype(xs.dtype)
    chunk_states = jnp.einsum("bckgn,bckgrp->bcgrpn", bs, xdt * decay_to_end[..., None])

    chunk_decay = jnp.exp(da_cum[..., -1]).transpose(3, 0, 1, 2)

    def step(state, inp):
        dec, new = inp
        return state * dec[..., None, None] + new, state

    init = jnp.zeros((b, N_GROUPS, R, P, D_STATE), jnp.float32)
    _, states_in = lax.scan(step, init, (chunk_decay, chunk_states.astype(jnp.float32).transpose(1, 0, 2, 3, 4, 5)))
    states_in = states_in.transpose(1, 0, 2, 3, 4, 5).astype(xs.dtype)

    decay_from_start = jnp.exp(da_cum).transpose(0, 3, 4, 1, 2).astype(xs.dtype)
    y_off = jnp.einsum("bcqgn,bcgrpn->bcqgrp", cs, states_in) * decay_from_start[..., None]
    y = (y_diag + y_off).reshape(b, nc * CHUNK, H, P)
    return y[:, pad:]


def gated_group_rmsnorm(y, z, g):
    u = (y * jax.nn.silu(z)).astype(jnp.float32)
    u = u.reshape(*u.shape[:-1], N_GROUPS, D_INNER // N_GROUPS)
    u = u * lax.rsqrt(jnp.mean(jnp.square(u), axis=-1, keepdims=True) + EPS)
    return (u.reshape(y.shape) * g.astype(jnp.float32)).astype(y.dtype)


def mixer_block(h, w_in, conv_a_w, conv_a_b, ln_a_g, ln_a_b, w_a_out,
                conv_b_w, conv_b_b, dt_bias, a_log, d_skip, norm_b_g, w_b_out, w_o):
    b, L, _ = h.shape
    proj = h @ w_in
    s1 = 2 * D_CONV
    s2 = s1 + D_INNER
    s3 = s2 + D_XBC
    s4 = s3 + N_SSM_HEADS
    a_in, z, xbc, dt, gates = jnp.split(proj, [s1, s2, s3, s4], axis=-1)

    a = a_in[..., :D_CONV] * jax.nn.sigmoid(a_in[..., D_CONV:])
    a = causal_dwconv(a, conv_a_w, conv_a_b)
    a = jax.nn.silu(layer_norm(a, ln_a_g, ln_a_b))
    y_a = a @ w_a_out

    xbc = jax.nn.silu(causal_dwconv(xbc, conv_b_w, conv_b_b))
    xs, bs, cs = jnp.split(xbc, [D_INNER, D_INNER + N_GROUPS * D_STATE], axis=-1)
    xs = xs.reshape(b, L, N_SSM_HEADS, HEAD_DIM)
    bs = bs.reshape(b, L, N_GROUPS, D_STATE)
    cs = cs.reshape(b, L, N_GROUPS, D_STATE)
    dt = jax.nn.softplus((dt + dt_bias).astype(jnp.float32))
    a_neg = -jnp.exp(a_log.astype(jnp.float32))
    y = ssd_chunked(xs, dt, a_neg, bs, cs) + d_skip[:, None].astype(xs.dtype) * xs
    y = gated_group_rmsnorm(y.reshape(b, L, D_INNER), z, norm_b_g)
    y_b = y @ w_b_out

    g = jax.nn.sigmoid(gates)
    merged = g[..., :D_MODEL] * y_a + g[..., D_MODEL:] * y_b
    return merged @ w_o


def conv_ffn(h, w_up, conv_f_w, conv_f_b, w_down):
    u = causal_dwconv(h @ w_up, conv_f_w, conv_f_b)
    act = jax.nn.silu(u[..., :D_FF]) * u[..., D_FF:]
    return act @ w_down


def setup_inputs(seed: int = 0) -> dict:
    key = jax.random.key(seed)
    ks = iter(jax.random.split(key, 32))
    f32 = jnp.float32

    def nrm(shape, scale):
        return jax.random.normal(next(ks), shape, f32) * scale

    def gain(shape):
        return 1.0 + 0.05 * jax.random.normal(next(ks), shape, f32)

    def bias(shape):
        return 0.02 * jax.random.normal(next(ks), shape, f32)

    dt0 = jnp.exp(jax.random.uniform(next(ks), (DEPTH, N_SSM_HEADS), f32, math.log(1e-3), math.log(1e-1)))
    dt_bias = dt0 + jnp.log(-jnp.expm1(-dt0))
    a_log = jnp.log(jax.random.uniform(next(ks), (DEPTH, N_SSM_HEADS), f32, 1.0, 16.0))

    return {
        "x": nrm((BATCH, SEQ, D_MODEL), 1.0),
        "meta_tokens": nrm((N_META, D_MODEL), 1.0),
        "ln_in_g": gain((D_MODEL,)),
        "ln_in_b": bias((D_MODEL,)),
        "w_in": nrm((DEPTH, D_MODEL, P_IN), D_MODEL ** -0.5),
        "conv_a_w": nrm((DEPTH, CONV_A_WIDTH, D_CONV), CONV_A_WIDTH ** -0.5),
        "conv_a_b": bias((DEPTH, D_CONV)),
        "ln_a_g": gain((DEPTH, D_CONV)),
        "ln_a_b": bias((DEPTH, D_CONV)),
        "w_a_out": nrm((DEPTH, D_CONV, D_MODEL), D_CONV ** -0.5),
        "conv_b_w": nrm((DEPTH, CONV_B_WIDTH, D_XBC), CONV_B_WIDTH ** -0.5),
        "conv_b_b": bias((DEPTH, D_XBC)),
        "dt_bias": dt_bias,
        "a_log": a_log,
        "d_skip": gain((DEPTH, N_SSM_HEADS)),
        "norm_b_g": gain((DEPTH, D_INNER)),
        "w_b_out": nrm((DEPTH, D_INNER, D_MODEL), D_INNER ** -0.5),
        "w_o": nrm((DEPTH, D_MODEL, D_MODEL), BETA * D_MODEL ** -0.5),
        "ln1_g": gain((DEPTH, D_MODEL)),
        "ln1_b": bias((DEPTH, D_MODEL)),
        "w_up": nrm((DEPTH, D_MODEL, 2 * D_FF), D_MODEL ** -0.5),
        "conv_f_w": nrm((DEPTH, CONV_F_WIDTH, 2 * D_FF), CONV_F_WIDTH ** -0.5),
        "conv_f_b": bias((DEPTH, 2 * D_FF)),
        "w_down": nrm((DEPTH, D_FF, D_MODEL), BETA * D_FF ** -0.5),
        "ln2_g": gain((DEPTH, D_MODEL)),
        "ln2_b": bias((DEPTH, D_MODEL)),
    }


def reference(x, meta_tokens, ln_in_g, ln_in_b, w_in, conv_a_w, conv_a_b, ln_a_g, ln_a_b, w_a_out,
              conv_b_w, conv_b_b, dt_bias, a_log, d_skip, norm_b_g, w_b_out, w_o, ln1_g, ln1_b,
              w_up, conv_f_w, conv_f_b, w_down, ln2_g, ln2_b):
    b = x.shape[0]
    meta = jnp.broadcast_to(meta_tokens[None].astype(x.dtype), (b, N_META, D_MODEL))
    h = jnp.concatenate([meta, x], axis=1)
    h = layer_norm(h, ln_in_g, ln_in_b)
    for i in range(DEPTH):
        mix = mixer_block(h, w_in[i], conv_a_w[i], conv_a_b[i], ln_a_g[i], ln_a_b[i], w_a_out[i],
                          conv_b_w[i], conv_b_b[i], dt_bias[i], a_log[i], d_skip[i], norm_b_g[i],
                          w_b_out[i], w_o[i])
        h = layer_norm(ALPHA * h + mix, ln1_g[i], ln1_b[i])
        ffn = conv_ffn(h, w_up[i], conv_f_w[i], conv_f_b[i], w_down[i])
        h = layer_norm(ALPHA * h + ffn, ln2_g[i], ln2_b[i])
    return h[:, N_META:]
```

```python
import contextlib
import numpy as np
import concourse.bass as bass
import concourse.mybir as mybir
from concourse.bass_utils import run_bass_kernel_spmd

F32 = mybir.dt.float32
BF16 = mybir.dt.bfloat16
AF = mybir.ActivationFunctionType
ALU = mybir.AluOpType

D = 2048
L = 2
SEQ = 2048
NMETA = 16
LT = SEQ + NMETA
TP = 576
EPS = 1e-5
ALPHA = (2.0 * L) ** 0.25
NBLK = 94
BLK = 8192


class Res:
    __slots__ = ("name", "w", "r", "excl")

    def __init__(self, name="", excl=False):
        self.name = name
        self.w = None
        self.r = []
        self.excl = excl


class Prog:
    ENGS = ("pe", "act", "dve", "pool", "sp")
    EPOCH = 30000

    def __init__(self, nc, n_chan=6, same_engine_sync=True):
        self.nc = nc
        self.ops = {e: [] for e in self.ENGS}
        self.cnt = {e: 0 for e in self.ENGS}
        self.seen = {e: {} for e in self.ENGS}
        self.same_engine_sync = same_engine_sync
        self.n_chan = n_chan
        self.chan_total = {}
        self.chan_rr = {e: 0 for e in self.ENGS}
        self.sems = {}
        self.last = {e: None for e in self.ENGS}

    def _deps(self, eng, reads, writes):
        deps = {}

        def add(d):
            if d is None:
                return
            k, v = d
            if deps.get(k, 0) < v:
                deps[k] = v
        for r in reads:
            add(r.w)
        for w in writes:
            add(w.w)
            for d in w.r:
                add(d)
        out = []
        for k, v in deps.items():
            if k[0] == eng and k[1] != "ch" and (eng == "pe" or not self.same_engine_sync):
                continue
            if self.seen[eng].get(k, 0) >= v:
                continue
            self.seen[eng][k] = v
            out.append((k, v))
        return out

    def _mark(self, done, reads, writes):
        for r in reads:
            r.r.append(done)
            if len(r.r) > 64:
                best = {}
                for k, v in r.r:
                    if best.get(k, 0) < v:
                        best[k] = v
                r.r = list(best.items())
        for w in writes:
            w.w = done
            w.r = []

    def op(self, eng, name, *args, r=(), w=(), **kw):
        reads, writes = list(r), list(w)
        writes += [x for x in reads if x.excl]
        reads = [x for x in reads if not x.excl]
        fn = (name, args, kw)
        waits = self._deps(eng, reads, writes)
        ep, idx = divmod(self.cnt[eng], self.EPOCH)
        self.cnt[eng] += 1
        done = ((eng, ep), idx + 1)
        self.last[eng] = done
        self.ops[eng].append(("op", fn, waits, (eng, ep)))
        self._mark(done, reads, writes)

    def dma(self, eng, out, in_, r=(), w=(), chan=None):
        reads, writes = list(r), list(w)
        fn = ("dma_start", (), {"out": out, "in_": in_})
        if chan is None:
            chan = (eng, "ch", self.chan_rr[eng] % self.n_chan)
            self.chan_rr[eng] += 1
        prev = self.chan_total.get(chan, 0)
        waits = self._deps(eng, reads, writes)
        if prev and self.seen[eng].get(chan, 0) < prev:
            self.seen[eng][chan] = prev
            waits.append((chan, prev))
        tot = prev + 16
        self.chan_total[chan] = tot
        self.ops[eng].append(("dma", fn, waits, chan))
        self._mark((chan, tot), reads, writes)

    def barrier(self):
        for e in self.ENGS:
            waits = []
            for k0 in self.ENGS:
                if k0 == e or self.last[k0] is None:
                    continue
                k, v = self.last[k0]
                if self.seen[e].get(k, 0) < v:
                    self.seen[e][k] = v
                    waits.append((k, v))
            for k, v in self.chan_total.items():
                if self.seen[e].get(k, 0) < v:
                    self.seen[e][k] = v
                    waits.append((k, v))
            if waits:
                self.ops[e].append(("wait", None, waits, None))

    def emit(self):
        nc = self.nc
        self.barrier()
        keys = []
        for e in self.ENGS:
            for ep in range((max(self.cnt[e], 1) - 1) // self.EPOCH + 1):
                keys.append((e, ep))
        keys += list(self.chan_total.keys())
        with contextlib.ExitStack() as st:
            for k in keys:
                nm = "s_" + "_".join(str(x) for x in k)
                self.sems[k] = st.enter_context(nc.semaphore(nm))
            block = st.enter_context(nc.Block())
            sems = self.sems

            def run(ename):
                def body(e):
                    for kind, fn, waits, chan in self.ops[ename]:
                        for k, v in waits:
                            e.wait_ge(sems[k], v)
                        if kind == "op":
                            getattr(e, fn[0])(*fn[1], **fn[2]).then_inc(sems[chan], 1)
                        elif kind == "dma":
                            getattr(e, fn[0])(*fn[1], **fn[2]).then_inc(sems[chan], 16)
                return body

            block.tensor(run("pe"))
            block.scalar(run("act"))
            block.vector(run("dve"))
            block.gpsimd(run("pool"))
            block.sync(run("sp"))


PO = {}
_o = 0
for _n, _w in (("cawk", 16 * 31), ("cab", 16), ("lag", 16), ("lab", 16), ("cbw", 48 * 4), ("cbb", 48),
               ("cfw", 88 * 3), ("cfb", 88), ("l1g", 16), ("l1b", 16), ("l2g", 16), ("l2b", 16),
               ("ling", 16), ("linb", 16), ("nbg", 32)):
    PO[_n] = _o
    _o += _w
NPAR = _o
P2 = {"dtb": 0, "alog": 64, "dsk": 128}
NP2 = 192
NS = 2
USE_WCACHE = False


def build_program(n_layers=L, n_quarters=4, dbg=None):
    nc = bass.Bass("TRN2", target_bir_lowering=False)
    xq = nc.dram_tensor("xq", [4, 128, 16 * TP], F32, kind="ExternalInput").ap()
    par_d = nc.dram_tensor("par", [L, 128, NPAR], F32, kind="ExternalInput").ap()
    par2_d = nc.dram_tensor("par2", [L, 64, NP2], F32, kind="ExternalInput").ap()
    wblk_d = nc.dram_tensor("wblk", [L, NBLK, 128, BLK], F32, kind="ExternalInput").ap()
    wdt_d = nc.dram_tensor("wdt", [L, 128, 16 * 64], F32, kind="ExternalInput").ap()
    yq = nc.dram_tensor("yq", [4, 128, 16 * 512], F32, kind="ExternalOutput").ap()
    hscr = nc.dram_tensor("hscr", [4, 128, 16 * TP], F32).ap()
    csscr = nc.dram_tensor("csscr", [9, 64, 64], F32).ap()
    sscr = nc.dram_tensor("sscr", [8, 128, 512], F32).ap()
    hpark = nc.dram_tensor("hpark", [128, 16 * TP], F32).ap()
    wcache = [nc.dram_tensor(f"wcache{i}", [NBLK, 128, BLK], BF16).ap() for i in range(L)]
    dbg_d = None
    if dbg is not None:
        dbg_d = nc.dram_tensor("dbg", [128, 16, TP], F32, kind="ExternalOutput").ap()

    P = Prog(nc)
    O = P.op
    with contextlib.ExitStack() as st:
        def sb(name, shape, dt):
            return st.enter_context(nc.sbuf_tensor(name, shape, dt))

        def RL(n, k):
            return [Res(f"{n}{i}") for i in range(k)]

        h = sb("h", [128, 16, TP], F32); h_r = RL("h", 16)
        hb = sb("hb", [128, 16, TP], BF16); hb_r = RL("hb", 16)
        wring = [sb(f"wr{i}", [128, BLK], BF16) for i in range(NS)]; wring_r = RL("wr", NS + 1)
        wring.append(h[:].rearrange("p a t -> p (a t)")[:, 4800:8896].bitcast(BF16))
        abuf = sb("abuf", [128, 16, TP], BF16); abuf_r = RL("abuf", 16)
        yn = sb("yn", [128, 32, TP], BF16); yn_r = RL("yn", 32)
        cab, cab_r = yn, yn_r
        mbuf, m_r = abuf, abuf_r
        S1 = sb("S1", [128, 512], F32); S_r = Res("S")
        Sb1 = sb("Sb1", [128, 512], BF16); Sb_r = Res("Sb")
        halo_a = sb("halo_a", [128, 16, 30], BF16); halo_a_r = RL("ha", 16)
        halo_b = sb("halo_b", [128, 48, 3], BF16); halo_b_r = RL("hbh", 48)
        halo_f = sb("halo_f", [128, 88, 2], BF16); halo_f_r = RL("hf", 88)
        par = sb("par_sb", [128, NPAR], F32); par_r = Res("par")
        par2 = sb("par2_sb", [64, NP2], F32); par2_r = Res("par2")
        parh = sb("parh", [128, 48], F32); parh_r = Res("parh")
        wdt = sb("wdt_sb", [128, 16, 64], BF16); wdt_r = Res("wdt")
        ones_bf = sb("ones_bf", [128, 128], BF16)
        ones_f = sb("ones_f", [64, 128], F32)
        identf = sb("identf", [128, 128], F32)
        ident = sb("ident", [128, 128], BF16)
        triu = sb("triu", [64, 64], F32)
        pmask = sb("pmask", [64, 1], F32)
        const_r = Res("const")
        dt_tok = sb("dt_tok", [64, 9, 64], F32); dt_r = Res("dt")
        cs_tok = sb("cs_tok", [64, 9, 64], F32); cs_r = Res("cs")
        dfs = sb("dfs", [64, 9, 64], F32); dfs_r = Res("dfs")
        dte = sb("dte", [64, 9, 64], F32); dte_r = Res("dte")
        cdec = sb("cdec", [128, 9, 64], F32); cdec_r = Res("cdec")
        scr46 = sb("scr46", [128, 1152], F32)
        cs_fm = scr46[0:64, 0:576].rearrange("p (c h) -> p c h", c=9); csfm_r = Res("csfm")
        sz_fm = scr46[:, :].bitcast(BF16).rearrange("p (c t) -> p c t", c=4); sz_r = RL("sz", 4)
        da_tok, da_r = cs_fm, csfm_r
        a_rep = sb("a_rep", [64, 64], F32); arep_r = Res("arep")
        csbc = [sb(f"csbc{i}", [64, 512], F32) for i in range(2)]; csbc_r = RL("csbc", 2)
        xs_fm = sb("xs_fm", [128, 4, TP], BF16); xs_r = RL("xs", 4)
        B_fm = sb("B_fm", [128, TP], BF16); B_r = Res("B")
        C_fm = sb("C_fm", [128, TP], BF16); C_r = Res("C")
        xin = [sb(f"xin{i}", [128, 3 + TP], BF16) for i in range(2)]; xin_r = RL("xin", 2)
        xs_tok = sb("xs_tok", [64, 512], BF16); xst_r = Res("xst")
        B_tok = sb("B_tok", [64, 128], BF16); bt_r = Res("bt")
        CBTm = sb("CBTm", [64, 64], F32); cbm_r = Res("cbm")

        MT = sb("MT", [64, 512], BF16); mt_r = Res("mt")
        xdt = sb("xdt", [64, 512], BF16); xdt_r = Res("xdt")
        xdte = sb("xdte", [64, 512], BF16); xdte_r = Res("xdte")
        dxs = sb("dxs", [64, 512], BF16); dxs_r = Res("dxs")
        yt = sb("yt", [64, 512], F32); yt_r = Res("yt")
        ynt = sb("ynt", [64, 512], BF16); ynt_r = Res("ynt")
        ss = sb("ss", [64, 2], F32); ss_r = Res("ss")

        tmpf = [sb(f"tmpf{i}", [128, 512], F32) for i in range(3)]; tmpf_r = RL("tmpf", 3)
        tmpb = [sb(f"tmpb{i}", [128, 512], BF16) for i in range(2)]; tmpb_r = RL("tmpb", 2)
        mean = sb("mean", [128, 512], F32); mean_r = Res("mean")
        rstd = sb("rstd", [128, 512], F32); rstd_r = Res("rstd")
        t2, t2_r = mean, mean_r
        diff, diff_r = rstd, rstd_r
        diag = [sb(f"diag{i}", [128, 128], BF16) for i in range(8)]; diag_r = RL("diag", 8)
        fin, fin_r = xin, xin_r
        pb = [st.enter_context(nc.psum_tensor(f"pb{i}", [128, 512], F32)) if i not in (3, 5) else None for i in range(8)]
        pbb = st.enter_context(nc.psum_tensor("pbb", [128, 1024], BF16))
        pbc = st.enter_context(nc.psum_tensor("pbc", [128, 1024], BF16))
        pb_r = [Res(f"pb{i}", excl=True) for i in range(8)]
        print("sbuf bytes remaining", nc.sbuf_bytes_remaining)

        cnt = {"tf": 0, "tb": 0, "dg": 0, "xin": 0, "fin": 0, "ck": 0}

        def nxt(key, n):
            i = cnt[key] % n
            cnt[key] += 1
            return i

        def init_consts():
            O("pool", "memset", ones_bf[:], 1.0, w=[const_r])
            O("pool", "memset", ones_f[:], 1.0, w=[const_r])
            O("pool", "memset", identf[:], 1.0, w=[const_r])
            O("pool", "affine_select", out=identf[:], in_=identf[:], pattern=[[-1, 128]], compare_op=ALU.is_equal,
              fill=0.0, base=0, channel_multiplier=1, r=[const_r], w=[const_r])
            O("pool", "tensor_copy", out=ident[:], in_=identf[:], r=[const_r], w=[const_r])
            O("pool", "memset", triu[:], 1.0, w=[const_r])
            O("pool", "affine_select", out=triu[:], in_=triu[:], pattern=[[1, 64]], compare_op=ALU.is_ge,
              fill=0.0, base=0, channel_multiplier=-1, r=[const_r], w=[const_r])
            O("pool", "memset", pmask[:], 1.0, w=[const_r])
            O("pool", "affine_select", out=pmask[:], in_=pmask[:], pattern=[[0, 1]], compare_op=ALU.is_ge,
              fill=0.0, base=-48, channel_multiplier=1, r=[const_r], w=[const_r])
            O("pool", "memset", hb[:], 0.0, w=hb_r)
            O("pool", "memset", abuf[:], 0.0, w=abuf_r)
            O("pool", "memset", yn[:], 0.0, w=yn_r)
            O("pool", "memset", h[:], 0.0, w=h_r)
            O("pool", "memset", xs_fm[:], 0.0, w=xs_r)
            O("pool", "memset", B_fm[:], 0.0, w=[B_r])
            O("pool", "memset", C_fm[:], 0.0, w=[C_r])
            for i in range(2):
                O("pool", "memset", xin[i][:], 0.0, w=[xin_r[i]])
            O("pool", "memset", ss[:], 0.0, w=[ss_r])

        wstate = {"issued": 0, "list": []}
        wuse = {"i": 0}

        wc_r = {}
        slot_of = {}
        hfree = {"v": False}

        def w_issue(cur):
            while wstate["issued"] < min(cur + 3, len(wstate["list"])):
                j = wstate["issued"]
                l, q, b = wstate["list"][j]
                live = {slot_of[k] for k in range(cur, j)}
                allowed = [0, 1, 2] if (b < 52 and hfree["v"]) else [0, 1]
                free = [x for x in allowed if x not in live]
                if not free:
                    break
                sl = free[0]
                slot_of[j] = sl
                extra = h_r if sl == 2 else []
                P.dma("pool", wring[sl][:] if sl < 2 else wring[sl], wblk_d[l, b, :, :], w=[wring_r[sl]] + extra, chan=("pool", "ch", "w%d" % sl))
                wstate["issued"] += 1

        def w_next():
            i = wuse["i"]
            wuse["i"] += 1
            w_issue(i)
            sl = slot_of[i]
            return (wring[sl] if sl < 2 else wring[sl]), wring_r[sl]

        def pcol(name, idx):
            o = PO[name] + idx
            return par[:, o:o + 1]

        def mm_group(banks, tiles, KC, lhsT_fn, rhs_fn, reads, M=128):
            for ti, (c0, n) in enumerate(tiles):
                bi = banks[ti]
                for k in range(KC):
                    O("pe", "matmul", pb[bi][0:M, 0:n], lhsT_fn(k), rhs_fn(k, c0, n), start=(k == 0), stop=(k == KC - 1),
                      r=reads, w=[pb_r[bi]])

        def layernorm(tiles, src_fn, src_r, KC, stat_banks, emit_out):
            b1, b2 = stat_banks
            inv = 1.0 / (KC * 128)
            for ti, (c0, n) in enumerate(tiles):
                for c in range(KC):
                    i1 = nxt("tb", 2)
                    O("act", "activation", out=tmpb[i1][:, 0:n], in_=src_fn(c, c0, n), func=AF.Copy, r=[src_r[c]], w=[tmpb_r[i1]])
                    O("pe", "matmul", pb[b1][:, 0:n], ones_bf[:], tmpb[i1][:, 0:n], start=(c == 0), stop=(c == KC - 1),
                      r=[tmpb_r[i1], const_r], w=[pb_r[b1]])
                    i2 = nxt("tb", 2)
                    O("dve", "tensor_tensor", out=tmpb[i2][:, 0:n], in0=src_fn(c, c0, n), in1=src_fn(c, c0, n), op=ALU.mult,
                      r=[src_r[c]], w=[tmpb_r[i2]])
                    O("pe", "matmul", pb[b2][:, 0:n], ones_bf[:], tmpb[i2][:, 0:n], start=(c == 0), stop=(c == KC - 1),
                      r=[tmpb_r[i2], const_r], w=[pb_r[b2]])
                O("act", "activation", out=mean[:, 0:n], in_=pb[b1][:, 0:n], func=AF.Copy, scale=inv, r=[pb_r[b1]], w=[mean_r])
                it = nxt("tf", 3)
                O("dve", "tensor_tensor", out=tmpf[it][:, 0:n], in0=mean[:, 0:n], in1=mean[:, 0:n], op=ALU.mult, r=[mean_r], w=[tmpf_r[it]])
                O("dve", "scalar_tensor_tensor", out=rstd[:, 0:n], in0=pb[b2][:, 0:n], scalar=inv, in1=tmpf[it][:, 0:n],
                  op0=ALU.mult, op1=ALU.subtract, r=[pb_r[b2], tmpf_r[it]], w=[rstd_r])
                O("dve", "tensor_scalar_max", out=rstd[:, 0:n], in0=rstd[:, 0:n], scalar1=0.0, r=[rstd_r], w=[rstd_r])
                O("act", "activation", out=rstd[:, 0:n], in_=rstd[:, 0:n], func=AF.Sqrt, bias=EPS, scale=1.0, r=[rstd_r], w=[rstd_r])
                O("dve", "reciprocal", out=rstd[:, 0:n], in_=rstd[:, 0:n], r=[rstd_r], w=[rstd_r])
                for c in range(KC):
                    it = nxt("tf", 3)
                    O("dve", "tensor_tensor", out=tmpf[it][:, 0:n], in0=src_fn(c, c0, n), in1=mean[:, 0:n], op=ALU.subtract,
                      r=[src_r[c], mean_r], w=[tmpf_r[it]])
                    O("dve", "tensor_tensor", out=tmpf[it][:, 0:n], in0=tmpf[it][:, 0:n], in1=rstd[:, 0:n], op=ALU.mult,
                      r=[rstd_r, tmpf_r[it]], w=[tmpf_r[it]])
                    emit_out(c, c0, n, tmpf[it][:, 0:n], tmpf_r[it])

        def conv_mm(bank, K, wname, chunk, src_fn, src_reads, c0, n):
            for k0 in range(0, K, 8):
                kk = list(range(k0, min(K, k0 + 8)))
                ids = []
                for k in kk:
                    di = nxt("dg", 8)
                    o = PO[wname] + chunk * K + k
                    O("dve", "tensor_scalar_mul", out=diag[di][:], in0=ident[:], scalar1=par[:, o:o + 1],
                      r=[const_r, par_r], w=[diag_r[di]])
                    ids.append(di)
                for k, di in zip(kk, ids):
                    O("pe", "matmul", pb[bank][:, 0:n], diag[di][:], src_fn(c0 - (K - 1) + k, n), start=(k == 0), stop=(k == K - 1),
                      r=[diag_r[di]] + src_reads, w=[pb_r[bank]])

        def act_ap(j, c0, n):
            return yn[:, j, c0:c0 + n] if j < 32 else abuf[:, j - 32, c0:c0 + n]

        def act_res(j):
            return yn_r[j] if j < 32 else abuf_r[j - 32]

        sscr_r = RL("sscr", 8)
        hscr_r = RL("hscr", 4)

        def quarter(l, q, last):
            r0 = 48 if q == 0 else 64
            T = TP - r0
            tiles = [(r0, T // 2), (r0 + T // 2, T // 2)] if q == 0 else [(64, 512)]
            nt = len(tiles)
            g0 = 0 if q == 0 else NMETA + 512 * q
            chunks = list(range(0, 9)) if q == 0 else list(range(1, 9))
            hbk = lambda k, c0, n: hb[:, k, c0:c0 + n]
            bsets = [[0], [1], [2], [4]] if nt == 1 else [[0, 1], [2, 4]]
            bctr = {"i": 0}

            def next_banks():
                b = bsets[bctr["i"] % len(bsets)]
                bctr["i"] += 1
                return b

            hflat = h[:].rearrange("p a t -> p (a t)")
            hpark_r = Res("hpark")
            if l == 0:
                hfree["v"] = False
                P.dma("sp", hflat, xq[q, :, :], w=h_r + [wring_r[2]])

                def out_in(c, c0, n, tn, tn_r):
                    O("act", "activation", out=h[:, c, c0:c0 + n], in_=tn, func=AF.Identity, scale=pcol("ling", c), bias=pcol("linb", c),
                      r=[tn_r, par_r], w=[h_r[c]])
                    O("act", "activation", out=hb[:, c, c0:c0 + n], in_=tn, func=AF.Identity, scale=pcol("ling", c), bias=pcol("linb", c),
                      r=[tn_r, par_r], w=[hb_r[c]])
                layernorm(tiles, lambda c, c0, n: h[:, c, c0:c0 + n], h_r, 16, (6, 7), out_in)
                P.dma("sp", hpark[:, :], hflat, r=h_r, w=[hpark_r])
                hfree["v"] = True
                hsrc, hsrc_r = hpark[:, :], hpark_r
            else:
                P.dma("pool", hb[:].rearrange("p a t -> p (a t)"), hscr[q, :, :], r=[hscr_r[q]], w=hb_r)
                hsrc, hsrc_r = hscr[q, :, :], hscr_r[q]
            if dbg is not None and dbg[0] == "h0" and (l, q) == dbg[1]:
                return "dump_h"

            for j in range(8):
                wv, wr = w_next()
                w3 = wv[:, :].rearrange("p (k c) -> p k c", k=16)
                for cc in range(2):
                    c = 2 * j + cc
                    ba = next_banks()
                    bb = next_banks()
                    mm_group(ba, tiles, 16, lambda k: w3[:, k, (2 * cc) * 128:(2 * cc + 1) * 128], hbk, [wr] + hb_r)
                    mm_group(bb, tiles, 16, lambda k: w3[:, k, (2 * cc + 1) * 128:(2 * cc + 2) * 128], hbk, [wr] + hb_r)
                    if q > 0:
                        O("pool", "tensor_copy", out=abuf[:, c, r0 - 30:r0], in_=halo_a[:, c, :], r=[halo_a_r[c]], w=[abuf_r[c]])
                    else:
                        O("pool", "memset", abuf[:, c, 0:r0], 0.0, w=[abuf_r[c]])
                    for ti, (c0, n) in enumerate(tiles):
                        it = nxt("tf", 3)
                        O("act", "activation", out=tmpf[it][:, 0:n], in_=pb[bb[ti]][:, 0:n], func=AF.Sigmoid, r=[pb_r[bb[ti]]], w=[tmpf_r[it]])
                        O("dve", "tensor_tensor", out=abuf[:, c, c0:c0 + n], in0=pb[ba[ti]][:, 0:n], in1=tmpf[it][:, 0:n], op=ALU.mult,
                          r=[pb_r[ba[ti]], tmpf_r[it]], w=[abuf_r[c]])
                    O("pool", "tensor_copy", out=halo_a[:, c, :], in_=abuf[:, c, TP - 30:TP], r=[abuf_r[c]], w=[halo_a_r[c]])
                    for ti, (c0, n) in enumerate(tiles):
                        bk = (6, 7)[cnt["ck"] % 2]
                        cnt["ck"] += 1
                        conv_mm(bk, 31, "cawk", c, lambda off, n_: abuf[:, c, off:off + n_], [abuf_r[c]], c0, n)
                        O("act", "activation", out=cab[:, c, c0:c0 + n], in_=pb[bk][:, 0:n], func=AF.Identity, bias=pcol("cab", c), scale=1.0,
                          r=[pb_r[bk], par_r], w=[cab_r[c]])

            if dbg is not None and dbg[0] == "a" and (l, q) == dbg[1]:
                return "dump_a"
            if dbg is not None and dbg[0] == "ca" and (l, q) == dbg[1]:
                return "dump_yn"

            def out_sa(c, c0, n, tn, tn_r):
                O("act", "activation", out=cab[:, c, c0:c0 + n], in_=tn, func=AF.Silu, scale=pcol("lag", c), bias=pcol("lab", c),
                  r=[tn_r, par_r], w=[cab_r[c]])
            layernorm(tiles, lambda c, c0, n: cab[:, c, c0:c0 + n], cab_r, 16, (6, 7), out_sa)

            def out_and_gate(KC, act_fn, act_reads, first):
                for jg in range(4):
                    gv, gr = w_next()
                    g3 = gv[:, :].rearrange("p (k c) -> p k c", k=16)
                    for cc in range(4):
                        bs = next_banks()
                        mm_group(bs, tiles, 16, lambda k: g3[:, k, cc * 128:(cc + 1) * 128], hbk, [gr] + hb_r)
                        for ti, (c0, n) in enumerate(tiles):
                            O("act", "activation", out=xs_fm[:, cc, c0:c0 + n], in_=pb[bs[ti]][:, 0:n], func=AF.Sigmoid,
                              r=[pb_r[bs[ti]]], w=[xs_r[cc]])
                    nb = 1 if KC == 16 else 2
                    ncol = 512 // nb
                    for b2 in range(nb):
                        wv, wr = w_next()
                        w3 = wv[:, :].rearrange("p (k c) -> p k c", k=KC)
                        for cc2 in range(ncol // 128):
                            cc = b2 * (ncol // 128) + cc2
                            c = jg * 4 + cc
                            bs = next_banks()
                            mm_group(bs, tiles, KC, lambda k: w3[:, k, cc2 * 128:(cc2 + 1) * 128], act_fn, [wr] + act_reads)
                            for ti, (c0, n) in enumerate(tiles):
                                if first:
                                    O("dve", "tensor_tensor", out=mbuf[:, c, c0:c0 + n], in0=pb[bs[ti]][:, 0:n], in1=xs_fm[:, cc, c0:c0 + n], op=ALU.mult,
                                      r=[pb_r[bs[ti]], xs_r[cc]], w=[m_r[c]])
                                else:
                                    it = nxt("tf", 3)
                                    O("dve", "tensor_tensor", out=tmpf[it][:, 0:n], in0=pb[bs[ti]][:, 0:n], in1=xs_fm[:, cc, c0:c0 + n], op=ALU.mult,
                                      r=[pb_r[bs[ti]], xs_r[cc]], w=[tmpf_r[it]])
                                    O("dve", "tensor_tensor", out=mbuf[:, c, c0:c0 + n], in0=tmpf[it][:, 0:n], in1=mbuf[:, c, c0:c0 + n], op=ALU.add,
                                      r=[tmpf_r[it], m_r[c]], w=[m_r[c]])
            def dt_stage():
                for ci in chunks:
                    for k in range(16):
                        O("pe", "matmul", pb[6][0:64, 0:64], hb[:, k, ci * 64:(ci + 1) * 64], wdt[:, k, :], start=(k == 0), stop=(k == 15),
                          r=hb_r + [wdt_r], w=[pb_r[6]])
                    O("dve", "tensor_tensor", out=dt_tok[:, ci, :], in0=pb[6][0:64, 0:64], in1=par2[0:64, P2["dtb"]:P2["dtb"] + 64], op=ALU.add,
                      r=[pb_r[6], par2_r], w=[dt_r])
                cl = slice(chunks[0], 9)
                ncl = 9 - chunks[0]
                O("dve", "tensor_scalar_min", out=dt_tok[:, cl, :], in0=dt_tok[:, cl, :], scalar1=60.0, r=[dt_r], w=[dt_r])
                O("act", "activation", out=dt_tok[:, cl, :], in_=dt_tok[:, cl, :], func=AF.Exp, r=[dt_r], w=[dt_r])
                O("act", "activation", out=dt_tok[:, cl, :], in_=dt_tok[:, cl, :], func=AF.Ln, bias=1.0, scale=1.0, r=[dt_r], w=[dt_r])
                if q == 0:
                    O("dve", "tensor_scalar_mul", out=dt_tok[:, 0, :], in0=dt_tok[:, 0, :], scalar1=pmask[:, 0:1], r=[dt_r, const_r], w=[dt_r])
                O("dve", "tensor_tensor", out=da_tok[:, cl, :], in0=dt_tok[:, cl, :], in1=a_rep[:, :].unsqueeze(1).to_broadcast([64, ncl, 64]),
                  op=ALU.mult, r=[dt_r, arep_r], w=[da_r])
                daf = scr46[0:64, 0:576]
                csf = cs_tok[:].rearrange("p c h -> p (c h)")
                cdf = cdec[:].rearrange("p c h -> p (c h)")
                dtf = dte[:].rearrange("p c h -> p (c h)")
                lo = chunks[0] * 64
                for (o, n) in ((lo, 288), (lo + 288, 576 - lo - 288)):
                    O("pe", "matmul", pb[6][0:64, 0:n], triu[:, :], daf[:, o:o + n], start=True, stop=True, r=[da_r, const_r], w=[pb_r[6]])
                    O("pe", "matmul", pb[7][:, 0:n], ones_f[:, :], daf[:, o:o + n], start=True, stop=True, r=[da_r, const_r], w=[pb_r[7]])
                    O("dve", "tensor_copy", out=csf[:, o:o + n], in_=pb[6][0:64, 0:n], r=[pb_r[6]], w=[cs_r])
                    O("act", "activation", out=cdf[:, o:o + n], in_=pb[7][:, 0:n], func=AF.Exp, r=[pb_r[7]], w=[cdec_r])
                    O("dve", "tensor_tensor", out=dtf[:, o:o + n], in0=pb[7][0:64, 0:n], in1=csf[:, o:o + n], op=ALU.subtract,
                      r=[pb_r[7], cs_r], w=[dte_r])
                O("act", "activation", out=dte[:, cl, :], in_=dte[:, cl, :], func=AF.Exp, r=[dte_r], w=[dte_r])
                O("act", "activation", out=dfs[:, cl, :], in_=cs_tok[:, cl, :], func=AF.Exp, r=[cs_r], w=[dfs_r])
                for ci in chunks:
                    O("pe", "transpose", pb[6][0:64, 0:64], cs_tok[:, ci, :], identf[0:64, 0:64], r=[cs_r, const_r, da_r], w=[pb_r[6]])
                    O("dve", "tensor_copy", out=cs_fm[:, ci, :], in_=pb[6][0:64, 0:64], r=[pb_r[6]], w=[csfm_r])
                csscr_r = Res("csscr")
                P.dma("sp", csscr[chunks[0]:9, :, :].rearrange("c h q -> h c q"), cs_fm[:, cl, :], r=[csfm_r], w=[csscr_r])

                return csscr_r

            csscr_r = dt_stage()
            if dbg is not None and dbg[0] == "sa" and (l, q) == dbg[1]:
                return "dump_yn"
            out_and_gate(16, lambda k, c0, n: cab[:, k, c0:c0 + n], cab_r, True)
            if dbg is not None and dbg[0] == "ma" and (l, q) == dbg[1]:
                return "dump_m"

            P.barrier()
            hraw = h[:].rearrange("p a t -> p (a t)")
            xs_fm2 = hraw[:, 0:1152].bitcast(BF16).rearrange("p (c t) -> p c t", c=4)
            sz_fm2 = hraw[:, 1152:2304].bitcast(BF16).rearrange("p (c t) -> p c t", c=4)
            bc2 = hraw[:, 2304:2880].bitcast(BF16)
            B_fm2, C_fm2 = bc2[:, 0:576], bc2[:, 576:1152]
            set2_r = dict(xs=RL("xs2_", 4), sz=RL("sz2_", 4), B=Res("B2"), C=Res("C2"))
            O("pool", "memset", xs_fm2[:, :, 0:64], 0.0, w=set2_r["xs"])
            O("pool", "memset", sz_fm2[:, :, 0:64], 0.0, w=set2_r["sz"])
            O("pool", "memset", B_fm2[:, 0:64], 0.0, w=[set2_r["B"]])
            O("pool", "memset", C_fm2[:, 0:64], 0.0, w=[set2_r["C"]])
            tb = hraw[:, 2880:4800]
            tbb = tb.bitcast(BF16)
            TS = [dict(xs_tok=xs_tok[:, :], B_tok=B_tok[:, :], CBTm=CBTm[:, :], diff=diff[0:64, :], MT=MT[:, :], xdt=xdt[:, :], xdte=xdte[:, :], dxs=dxs[:, :],
                       xst_r=xst_r, bt_r=bt_r, cbm_r=cbm_r, diff_r=diff_r, mt_r=mt_r, xdt_r=xdt_r, xdte_r=xdte_r, dxs_r=dxs_r),
                  dict(xs_tok=tbb[0:64, 0:512], B_tok=tbb[0:64, 512:640], MT=tbb[0:64, 640:1152], xdt=tbb[0:64, 1152:1664],
                       xdte=tbb[0:64, 1664:2176], dxs=tbb[0:64, 2176:2688], CBTm=tb[0:64, 1344:1408], diff=tb[0:64, 1408:1920],
                       xst_r=Res("xst2"), bt_r=Res("bt2"), cbm_r=Res("cbm2"), diff_r=Res("diff2"), mt_r=Res("mt2"), xdt_r=Res("xdt2"),
                       xdte_r=Res("xdte2"), dxs_r=Res("dxs2"))]
            GS = [dict(xs=xs_fm, sz=sz_fm, B=B_fm, C=C_fm, xs_r=xs_r, sz_r=sz_r, B_r=B_r, C_r=C_r),
                  dict(xs=xs_fm2, sz=sz_fm2, B=B_fm2, C=C_fm2, xs_r=set2_r["xs"], sz_r=set2_r["sz"], B_r=set2_r["B"], C_r=set2_r["C"])]
            def stop_at(pt):
                return dbg is not None and dbg[0] == "stop" and dbg[2] == pt and (l, q) == dbg[1]
            if stop_at(1):
                return "dump_h"
            P.barrier()
            def conv_chunk(wa, wres, col, pidx, dst_fn, dst_r):
                ba = [0, 1][:nt]
                mm_group(ba, tiles, 16, lambda k: wa[:, k, col * 128:(col + 1) * 128], hbk, [wres] + hb_r)
                xi = nxt("xin", 2)
                if q > 0:
                    O("pool", "tensor_copy", out=xin[xi][:, r0:r0 + 3], in_=halo_b[:, pidx, :], r=[halo_b_r[pidx]], w=[xin_r[xi]])
                else:
                    O("pool", "memset", xin[xi][:, r0:r0 + 3], 0.0, w=[xin_r[xi]])
                for ti, (c0, n) in enumerate(tiles):
                    O("act", "activation", out=xin[xi][:, 3 + c0:3 + c0 + n], in_=pb[ba[ti]][:, 0:n], func=AF.Copy, r=[pb_r[ba[ti]]], w=[xin_r[xi]])
                O("pool", "tensor_copy", out=halo_b[:, pidx, :], in_=xin[xi][:, TP:TP + 3], r=[xin_r[xi]], w=[halo_b_r[pidx]])
                for ti, (c0, n) in enumerate(tiles):
                    conv_mm(2, 4, "cbw", pidx, lambda off, n_: xin[xi][:, 3 + off:3 + off + n_], [xin_r[xi]], c0, n)
                    i1 = nxt("tf", 3)
                    O("act", "activation", out=tmpf[i1][:, 0:n], in_=pb[2][:, 0:n], func=AF.Tanh, bias=parh[:, pidx:pidx + 1], scale=0.5,
                      r=[pb_r[2], parh_r], w=[tmpf_r[i1]])
                    i2 = nxt("tf", 3)
                    O("dve", "tensor_scalar", out=tmpf[i2][:, 0:n], in0=pb[2][:, 0:n], scalar1=pcol("cbb", pidx), scalar2=0.5, op0=ALU.add, op1=ALU.mult,
                      r=[pb_r[2], par_r], w=[tmpf_r[i2]])
                    O("dve", "scalar_tensor_tensor", out=dst_fn(c0, n), in0=tmpf[i1][:, 0:n], scalar=1.0, in1=tmpf[i2][:, 0:n], op0=ALU.add, op1=ALU.mult,
                      r=[tmpf_r[i1], tmpf_r[i2]], w=[dst_r])

            def prologue_pieces(g):
                G = GS[g % 2]
                wst = {}

                def p_xs(cc):
                    def f():
                        if cc == 0:
                            wv, wr = w_next()
                            wst["w3"] = wv[:, :].rearrange("p (k c) -> p k c", k=16); wst["wr"] = wr
                        conv_chunk(wst["w3"], wst["wr"], cc, g * 4 + cc, lambda c0, n: G["xs"][:, cc, c0:c0 + n], G["xs_r"][cc])
                    return f

                def p_bc(which):
                    def f():
                        if which == 0:
                            wv2, wr2 = w_next()
                            wst["w32"] = wv2[:, 0:4096].rearrange("p (k c) -> p k c", k=16); wst["wr2"] = wr2
                            conv_chunk(wst["w32"], wst["wr2"], 0, 32 + g, lambda c0, n: G["B"][:, c0:c0 + n], G["B_r"])
                        else:
                            conv_chunk(wst["w32"], wst["wr2"], 1, 40 + g, lambda c0, n: G["C"][:, c0:c0 + n], G["C_r"])
                    return f

                def p_z(cc):
                    def f():
                        if cc == 0:
                            wv3, wr3 = w_next()
                            wst["wz"] = wv3[:, :].rearrange("p (k c) -> p k c", k=16); wst["wr3"] = wr3
                        wz, wr3 = wst["wz"], wst["wr3"]
                        bs = [0, 1][:nt]
                        mm_group(bs, tiles, 16, lambda k: wz[:, k, cc * 128:(cc + 1) * 128], hbk, [wr3] + hb_r)
                        for ti, (c0, n) in enumerate(tiles):
                            i1 = nxt("tf", 3)
                            O("act", "activation", out=tmpf[i1][:, 0:n], in_=pb[bs[ti]][:, 0:n], func=AF.Tanh, scale=0.5, r=[pb_r[bs[ti]]], w=[tmpf_r[i1]])
                            O("dve", "scalar_tensor_tensor", out=G["sz"][:, cc, c0:c0 + n], in0=tmpf[i1][:, 0:n], scalar=1.0, in1=pb[bs[ti]][:, 0:n],
                              op0=ALU.add, op1=ALU.mult, r=[tmpf_r[i1], pb_r[bs[ti]]], w=[G["sz_r"][cc]])
                    return f
                return [p_xs(0), p_xs(1), p_xs(2), p_xs(3), p_bc(0), p_bc(1), p_z(0), p_z(1), p_z(2), p_z(3)]

            pend = prologue_pieces(0)
            for g in range(8):
                for f in pend:
                    f()
                pend = prologue_pieces(g + 1) if g < 7 else []
                per = -(-len(pend) // len(chunks)) if pend else 0
                G = GS[g % 2]
                xs_fmG, sz_fmG, B_fmG, C_fmG = G["xs"], G["sz"], G["B"], G["C"]
                xsG_r, szG_r, BG_r, CG_r = G["xs_r"], G["sz_r"], G["B_r"], G["C_r"]
                hs = slice(g * 8, g * 8 + 8)
                if q == 0:
                    O("pool", "memset", S1[:, :], 0.0, w=[S_r])
                else:
                    P.dma("sp", S1[:, :], sscr[g, :, :], r=[sscr_r[g]], w=[S_r])
                O("act", "activation", out=Sb1[:, :], in_=S1[:, :], func=AF.Copy, r=[S_r], w=[Sb_r])
                if stop_at(2):
                    return "dump_h"
                xsT = pbb[0:64, 0:640]
                ynT = pbb[:, 512:1024]
                szT = pbc[0:64, 0:512]

                def stage_a(ci):
                    T = TS[ci % 2]
                    tk = slice(ci * 64, ci * 64 + 64)
                    cb = ci % 2
                    P.dma("sp", csbc[cb][:, :], csscr[ci, g * 8:g * 8 + 8, :].rearrange("h q -> (h q)").partition_broadcast(64),
                          r=[csscr_r], w=[csbc_r[cb]])
                    for cc in range(4):
                        O("pe", "transpose", xsT[:, cc * 128:(cc + 1) * 128], xs_fmG[:, cc, tk], ident[:, :], r=[xsG_r[cc], const_r], w=[pb_r[5]])
                    O("pe", "transpose", xsT[:, 512:640], B_fmG[:, tk], ident[:, :], r=[BG_r, const_r], w=[pb_r[5]])
                    O("act", "activation", out=T["xs_tok"], in_=xsT[:, 0:512], func=AF.Copy, r=[pb_r[5]], w=[T["xst_r"]])
                    O("act", "activation", out=T["B_tok"], in_=xsT[:, 512:640], func=AF.Copy, r=[pb_r[5]], w=[T["bt_r"]])
                    O("pe", "matmul", pb[4][0:64, 0:64], B_fmG[:, tk], C_fmG[:, tk], start=True, stop=True, r=[BG_r, CG_r], w=[pb_r[4]])
                    O("dve", "tensor_tensor", out=T["CBTm"], in0=pb[4][0:64, 0:64], in1=triu[:, :], op=ALU.mult, r=[pb_r[4], const_r], w=[T["cbm_r"]])
                    d3 = T["diff"].rearrange("p (h q) -> p h q", h=8)
                    O("pool", "tensor_tensor", out=d3, in0=csbc[cb][:].rearrange("p (h q) -> p h q", h=8),
                      in1=cs_tok[:, ci, hs].unsqueeze(2).to_broadcast([64, 8, 64]), op=ALU.subtract, r=[csbc_r[cb], cs_r], w=[T["diff_r"]])
                    O("act", "activation", out=T["diff"], in_=T["diff"], func=AF.Exp, r=[T["diff_r"]], w=[T["diff_r"]])
                    O("dve", "scalar_tensor_tensor", out=T["MT"].rearrange("p (h q) -> p h q", h=8), in0=d3, scalar=1.0,
                      in1=T["CBTm"].unsqueeze(1).to_broadcast([64, 8, 64]), op0=ALU.min, op1=ALU.mult, r=[T["diff_r"], T["cbm_r"]], w=[T["mt_r"]])
                    x3 = T["xs_tok"].rearrange("p (h q) -> p h q", h=8)
                    O("pool", "tensor_tensor", out=T["xdt"].rearrange("p (h q) -> p h q", h=8), in0=x3,
                      in1=dt_tok[:, ci, hs].unsqueeze(2).to_broadcast([64, 8, 64]), op=ALU.mult, r=[T["xst_r"], dt_r], w=[T["xdt_r"]])
                    O("pool", "tensor_tensor", out=T["xdte"].rearrange("p (h q) -> p h q", h=8), in0=T["xdt"].rearrange("p (h q) -> p h q", h=8),
                      in1=dte[:, ci, hs].unsqueeze(2).to_broadcast([64, 8, 64]), op=ALU.mult, r=[T["xdt_r"], dte_r], w=[T["xdte_r"]])
                    O("pool", "tensor_tensor", out=T["dxs"].rearrange("p (h q) -> p h q", h=8), in0=x3,
                      in1=par2[0:64, P2["dsk"] + g * 8:P2["dsk"] + g * 8 + 8].unsqueeze(2).to_broadcast([64, 8, 64]), op=ALU.mult,
                      r=[T["xst_r"], par2_r], w=[T["dxs_r"]])

                def stage_b(ci):
                    T = TS[ci % 2]
                    tk = slice(ci * 64, ci * 64 + 64)
                    for r in range(8):
                        rs = slice(r * 64, (r + 1) * 64)
                        O("pe", "matmul", pb[6][0:64, rs], T["MT"][:, rs], T["xdt"][:, rs], start=True, stop=False, r=[T["mt_r"], T["xdt_r"]], w=[pb_r[6]])
                        O("pe", "matmul", pb[6][0:64, rs], ident[0:64, 0:64], T["dxs"][:, rs], start=False, stop=True, r=[T["dxs_r"], const_r], w=[pb_r[6]])
                    O("pe", "matmul", pb[7][0:64, 0:512], C_fmG[:, tk], Sb1[:, :], start=True, stop=True, r=[CG_r, Sb_r], w=[pb_r[7]])
                    y3 = yt[:].rearrange("p (h q) -> p h q", h=8)
                    O("dve", "tensor_tensor", out=y3, in0=pb[7][0:64, 0:512].rearrange("p (h q) -> p h q", h=8),
                      in1=dfs[:, ci, hs].unsqueeze(2).to_broadcast([64, 8, 64]), op=ALU.mult, r=[pb_r[7], dfs_r], w=[yt_r])
                    O("dve", "tensor_tensor", out=yt[:, :], in0=yt[:, :], in1=pb[6][0:64, 0:512], op=ALU.add, r=[pb_r[6], yt_r], w=[yt_r])
                    O("pe", "matmul", pb[1][:, 0:512], T["B_tok"], T["xdte"], start=True, stop=True, r=[T["bt_r"], T["xdte_r"]], w=[pb_r[1]])
                    O("pool", "tensor_tensor", out=t2[:].rearrange("p (h q) -> p h q", h=8), in0=S1[:].rearrange("p (h q) -> p h q", h=8),
                      in1=cdec[:, ci, hs].unsqueeze(2).to_broadcast([128, 8, 64]), op=ALU.mult, r=[S_r, cdec_r], w=[t2_r])
                    O("dve", "tensor_tensor", out=S1[:, :], in0=t2[:, :], in1=pb[1][:, 0:512], op=ALU.add, r=[t2_r, pb_r[1]], w=[S_r])
                    O("act", "activation", out=Sb1[:, :], in_=S1[:, :], func=AF.Copy, r=[S_r], w=[Sb_r])

                def stage_c(ci):
                    tk = slice(ci * 64, ci * 64 + 64)
                    for cc in range(4):
                        O("pe", "transpose", szT[:, cc * 128:(cc + 1) * 128], sz_fmG[:, cc, tk], ident[:, :], r=[szG_r[cc], const_r], w=[pb_r[3]])
                    O("dve", "tensor_tensor", out=ynt[:, :], in0=yt[:, :], in1=szT[:, :], op=ALU.mult, r=[pb_r[3], yt_r], w=[ynt_r])

                def stage_c2(ci):
                    tk = slice(ci * 64, ci * 64 + 64)
                    for cc in range(4):
                        O("pe", "transpose", ynT[:, 128 + cc * 64:128 + (cc + 1) * 64], ynt[:, cc * 128:(cc + 1) * 128], ident[0:64, 0:64],
                          r=[ynt_r, const_r], w=[pb_r[5]])
                    O("act", "activation", out=yn[:, g * 4:g * 4 + 4, tk], in_=ynT[:, 128:384].rearrange("p (c t) -> p c t", c=4), func=AF.Copy,
                      r=[pb_r[5]], w=yn_r[g * 4:g * 4 + 4])

                def group_norm():
                    for ti, (c0, n) in enumerate(tiles):
                        for cc in range(4):
                            i2 = nxt("tb", 2)
                            O("dve", "tensor_tensor", out=tmpb[i2][:, 0:n], in0=yn[:, g * 4 + cc, c0:c0 + n], in1=yn[:, g * 4 + cc, c0:c0 + n], op=ALU.mult,
                              r=[yn_r[g * 4 + cc]], w=[tmpb_r[i2]])
                            O("pe", "matmul", pb[4][:, 0:n], ones_bf[:], tmpb[i2][:, 0:n], start=(cc == 0), stop=(cc == 3),
                              r=[tmpb_r[i2], const_r], w=[pb_r[4]])
                        it = nxt("tf", 3)
                        O("act", "activation", out=tmpf[it][:, 0:n], in_=pb[4][:, 0:n], func=AF.Sqrt, bias=4.0 * EPS, scale=1.0 / 512.0, r=[pb_r[4]], w=[tmpf_r[it]])
                        O("dve", "reciprocal", out=tmpf[it][:, 0:n], in_=tmpf[it][:, 0:n], r=[tmpf_r[it]], w=[tmpf_r[it]])
                        for cc in range(4):
                            o = PO["nbg"] + g * 4 + cc
                            O("dve", "scalar_tensor_tensor", out=yn[:, g * 4 + cc, c0:c0 + n], in0=yn[:, g * 4 + cc, c0:c0 + n], scalar=par[:, o:o + 1],
                              in1=tmpf[it][:, 0:n], op0=ALU.mult, op1=ALU.mult, r=[yn_r[g * 4 + cc], tmpf_r[it], par_r], w=[yn_r[g * 4 + cc]])

                stage_a(chunks[0])
                npc, nch, done = len(pend), len(chunks), 0
                for idx, ci in enumerate(chunks):
                    if idx + 1 < nch:
                        stage_a(chunks[idx + 1])
                    stage_b(ci)
                    if idx > 0:
                        stage_c2(chunks[idx - 1])
                    want = ((idx + 1) * npc + nch - 1) // nch
                    while done < want and pend:
                        pend.pop(0)()
                        done += 1
                    stage_c(ci)
                stage_c2(chunks[-1])
                group_norm()
                P.dma("sp", sscr[g, :, :], S1[:, :], r=[S_r], w=[sscr_r[g]])
            if dbg is not None and dbg[0] == "yn" and (l, q) == dbg[1]:
                return "dump_yn"

            P.barrier()
            out_and_gate(32, lambda k, c0, n: yn[:, k, c0:c0 + n], yn_r, False)

            hfree["v"] = False
            P.dma("sp", hflat, hsrc, r=[hsrc_r], w=h_r + [wring_r[2]])
            for jb in range(4):
                wv, wr = w_next()
                w3 = wv[:, :].rearrange("p (k c) -> p k c", k=16)
                for cc in range(4):
                    c = jb * 4 + cc
                    ba = next_banks()
                    mm_group(ba, tiles, 16, lambda k: w3[:, k, cc * 128:(cc + 1) * 128], lambda k, c0, n: mbuf[:, k, c0:c0 + n], [wr] + m_r)
                    for ti, (c0, n) in enumerate(tiles):
                        O("dve", "scalar_tensor_tensor", out=h[:, c, c0:c0 + n], in0=h[:, c, c0:c0 + n], scalar=ALPHA, in1=pb[ba[ti]][:, 0:n],
                          op0=ALU.mult, op1=ALU.add, r=[pb_r[ba[ti]], h_r[c]], w=[h_r[c]])

            def out_ln(gname, bname, with_hb=False):
                def f(c, c0, n, tn, tn_r):
                    O("act", "activation", out=h[:, c, c0:c0 + n], in_=tn, func=AF.Identity, scale=pcol(gname, c), bias=pcol(bname, c),
                      r=[tn_r, par_r], w=[h_r[c]])
                    if with_hb:
                        O("act", "activation", out=hb[:, c, c0:c0 + n], in_=tn, func=AF.Identity, scale=pcol(gname, c), bias=pcol(bname, c),
                          r=[tn_r, par_r], w=[hb_r[c]])
                return f
            layernorm(tiles, lambda c, c0, n: h[:, c, c0:c0 + n], h_r, 16, (6, 7), out_ln("l1g", "l1b", True))
            if dbg is not None and dbg[0] == "h1" and (l, q) == dbg[1]:
                return "dump_h"

            P.barrier()
            for jb in range(22):
                wv, wr = w_next()
                w3 = wv[:, :].rearrange("p (k c) -> p k c", k=16)
                for cc in range(2):
                    j = 2 * jb + cc
                    outs = []
                    for half in range(2):
                        pidx = j + 44 * half
                        ba = next_banks()
                        mm_group(ba, tiles, 16, lambda k: w3[:, k, (2 * cc + half) * 128:(2 * cc + half + 1) * 128], hbk, [wr] + hb_r)
                        fi = half
                        if q > 0:
                            O("pool", "tensor_copy", out=fin[fi][:, r0:r0 + 2], in_=halo_f[:, pidx, :], r=[halo_f_r[pidx]], w=[fin_r[fi]])
                        else:
                            O("pool", "memset", fin[fi][:, r0:r0 + 2], 0.0, w=[fin_r[fi]])
                        for ti, (c0, n) in enumerate(tiles):
                            O("act", "activation", out=fin[fi][:, 2 + c0:2 + c0 + n], in_=pb[ba[ti]][:, 0:n], func=AF.Copy, r=[pb_r[ba[ti]]], w=[fin_r[fi]])
                        O("pool", "tensor_copy", out=halo_f[:, pidx, :], in_=fin[fi][:, TP:TP + 2], r=[fin_r[fi]], w=[halo_f_r[pidx]])
                        outs.append(fi)
                    for ti, (c0, n) in enumerate(tiles):
                        gi = nxt("tf", 3)
                        conv_mm(6, 3, "cfw", j, lambda off, n_: fin[0][:, 2 + off:2 + off + n_], [fin_r[0]], c0, n)
                        O("act", "activation", out=tmpf[gi][:, 0:n], in_=pb[6][:, 0:n], func=AF.Silu, bias=pcol("cfb", j), scale=1.0,
                          r=[pb_r[6], par_r], w=[tmpf_r[gi]])
                        conv_mm(7, 3, "cfw", j + 44, lambda off, n_: fin[1][:, 2 + off:2 + off + n_], [fin_r[1]], c0, n)
                        O("dve", "scalar_tensor_tensor", out=act_ap(j, c0, n), in0=pb[7][:, 0:n], scalar=pcol("cfb", j + 44), in1=tmpf[gi][:, 0:n],
                          op0=ALU.add, op1=ALU.mult, r=[pb_r[7], tmpf_r[gi], par_r], w=[act_res(j)])
            act_reads = yn_r + abuf_r[0:12]
            for c in range(16):
                wv, wr = w_next()
                w3 = wv[:, 0:44 * 128].rearrange("p (k c) -> p k c", k=44)
                ba = next_banks()
                mm_group(ba, tiles, 44, lambda k: w3[:, k, :], lambda k, c0, n: act_ap(k, c0, n), [wr] + act_reads)
                for ti, (c0, n) in enumerate(tiles):
                    O("dve", "scalar_tensor_tensor", out=h[:, c, c0:c0 + n], in0=h[:, c, c0:c0 + n], scalar=ALPHA, in1=pb[ba[ti]][:, 0:n],
                      op0=ALU.mult, op1=ALU.add, r=[pb_r[ba[ti]], h_r[c]], w=[h_r[c]])
            layernorm(tiles, lambda c, c0, n: h[:, c, c0:c0 + n], h_r, 16, (6, 7), out_ln("l2g", "l2b"))
            if last:
                P.dma("sp", yq[q, :, :].rearrange("p (c t) -> p c t", c=16), h[:, :, 64:TP], r=h_r)
                hfree["v"] = True
            else:
                O("pool", "memset", h[:, :, 0:r0], 0.0, r=h_r, w=h_r)
                P.dma("sp", hscr[q, :, :], hflat, r=h_r, w=[hscr_r[q]])
                hfree["v"] = True
            return None

        init_consts()
        for l in range(n_layers):
            for q in range(n_quarters):
                for b in range(NBLK):
                    wstate["list"].append((l, q, b))
        stop = False
        for l in range(n_layers):
            if stop:
                break
            P.barrier()
            P.dma("sp", par[:, :], par_d[l, :, :], w=[par_r])
            P.dma("sp", par2[:, :], par2_d[l, :, :], w=[par2_r])
            P.dma("pool", wdt[:].rearrange("p k c -> p (k c)"), wdt_d[l, :, :], w=[wdt_r])
            O("dve", "tensor_scalar_mul", out=parh[:, :], in0=par[:, PO["cbb"]:PO["cbb"] + 48], scalar1=0.5, r=[par_r], w=[parh_r])
            O("act", "activation", out=a_rep[:, :], in_=par2[0:64, P2["alog"]:P2["alog"] + 64], func=AF.Exp, r=[par2_r], w=[arep_r])
            O("dve", "tensor_scalar_mul", out=a_rep[:, :], in0=a_rep[:, :], scalar1=-1.0, r=[arep_r], w=[arep_r])
            for q in range(n_quarters):
                rc = quarter(l, q, last=(l == n_layers - 1))
                if rc is not None:
                    P.barrier()
                    if rc == "dump_a":
                        for c in range(16):
                            O("dve", "tensor_copy", out=h[:, c, :], in_=abuf[:, c, :], r=[abuf_r[c]], w=[h_r[c]])
                            P.dma("sp", dbg_d[:, c, :], h[:, c, :], r=[h_r[c]])
                    elif rc == "dump_h":
                        for c in range(16):
                            P.dma("sp", dbg_d[:, c, :], h[:, c, :], r=[h_r[c]])
                    elif rc == "dump_m":
                        for c in range(16):
                            O("dve", "tensor_copy", out=h[:, c, :], in_=mbuf[:, c, :], r=[m_r[c]], w=[h_r[c]])
                            P.dma("sp", dbg_d[:, c, :], h[:, c, :], r=[h_r[c]])
                    elif rc == "dump_yn":
                        for c in range(16):
                            O("dve", "tensor_copy", out=h[:, c, :], in_=yn[:, dbg[2] + c, :], r=[yn_r[dbg[2] + c]], w=[h_r[c]])
                            P.dma("sp", dbg_d[:, c, :], h[:, c, :], r=[h_r[c]])
                    stop = True
                    break
        P.emit()
    return nc


def _col(v, n):
    return np.ascontiguousarray(v.reshape(n, 128).T)


def _blk(w, KC, cols):
    sub = w[:, cols]
    t = sub.reshape(KC, 128, len(cols)).transpose(1, 0, 2).reshape(128, KC * len(cols))
    out = np.zeros((128, BLK), np.float32)
    out[:, :t.shape[1]] = t
    return out


def prep_layer(i, inp):
    w_in = inp["w_in"][i]
    ar = np.arange
    blocks = []
    A1, A2, Z0, X0, DT0, G0 = 0, 2048, 4096, 8192, 14336, 14400
    for j in range(8):
        cols = np.concatenate([A1 + (2 * j) * 128 + ar(128), A2 + (2 * j) * 128 + ar(128),
                               A1 + (2 * j + 1) * 128 + ar(128), A2 + (2 * j + 1) * 128 + ar(128)])
        blocks.append(_blk(w_in, 16, cols))
    for jb in range(4):
        blocks.append(_blk(w_in, 16, G0 + jb * 512 + ar(512)))
        blocks.append(_blk(inp["w_a_out"][i], 16, jb * 512 + ar(512)))
    for g in range(8):
        blocks.append(_blk(w_in, 16, X0 + g * 512 + ar(512)))
        cols = np.concatenate([X0 + 4096 + g * 128 + ar(128), X0 + 4096 + 1024 + g * 128 + ar(128)])
        blocks.append(_blk(w_in, 16, cols))
        blocks.append(_blk(w_in, 16, Z0 + g * 512 + ar(512)))
    for jb in range(8):
        if jb % 2 == 0:
            blocks.append(_blk(w_in, 16, G0 + 2048 + (jb // 2) * 512 + ar(512)))
        blocks.append(_blk(inp["w_b_out"][i], 32, jb * 256 + ar(256)))
    for jb in range(4):
        blocks.append(_blk(inp["w_o"][i], 16, jb * 512 + ar(512)))
    w_up = inp["w_up"][i]
    for jb in range(22):
        cols = np.concatenate([(2 * jb) * 128 + ar(128), 5632 + (2 * jb) * 128 + ar(128),
                               (2 * jb + 1) * 128 + ar(128), 5632 + (2 * jb + 1) * 128 + ar(128)])
        blocks.append(_blk(w_up, 16, cols))
    for c in range(16):
        blocks.append(_blk(inp["w_down"][i], 44, c * 128 + ar(128)))
    assert len(blocks) == NBLK, len(blocks)
    wblk = np.stack(blocks)
    wdt = w_in[:, DT0:DT0 + 64].reshape(16, 128, 64).transpose(1, 0, 2).reshape(128, 16 * 64)
    par = np.zeros((128, NPAR), np.float32)

    def put(name, arr):
        a = arr.reshape(128, -1)
        par[:, PO[name]:PO[name] + a.shape[1]] = a
    put("cawk", inp["conv_a_w"][i].T.reshape(16, 128, 31).transpose(1, 0, 2))
    put("cab", _col(inp["conv_a_b"][i], 16))
    put("lag", _col(inp["ln_a_g"][i], 16))
    put("lab", _col(inp["ln_a_b"][i], 16))
    put("cbw", inp["conv_b_w"][i].T.reshape(48, 128, 4).transpose(1, 0, 2))
    put("cbb", _col(inp["conv_b_b"][i], 48))
    put("cfw", inp["conv_f_w"][i].T.reshape(88, 128, 3).transpose(1, 0, 2))
    put("cfb", _col(inp["conv_f_b"][i], 88))
    put("l1g", _col(inp["ln1_g"][i], 16))
    put("l1b", _col(inp["ln1_b"][i], 16))
    put("l2g", _col(inp["ln2_g"][i], 16))
    put("l2b", _col(inp["ln2_b"][i], 16))
    put("ling", _col(inp["ln_in_g"], 16))
    put("linb", _col(inp["ln_in_b"], 16))
    put("nbg", _col(inp["norm_b_g"][i], 32))
    par2 = np.zeros((64, NP2), np.float32)
    par2[:, P2["dtb"]:P2["dtb"] + 64] = inp["dt_bias"][i][None, :]
    par2[:, P2["alog"]:P2["alog"] + 64] = inp["a_log"][i][None, :]
    par2[:, P2["dsk"]:P2["dsk"] + 64] = inp["d_skip"][i][None, :]
    return wblk, np.ascontiguousarray(wdt), par, par2


def prep_x(x_b, meta):
    hc = np.concatenate([meta, x_b], axis=0)
    hT = hc.T.reshape(16, 128, LT).transpose(1, 0, 2)
    xq = np.zeros((4, 128, 16, TP), np.float32)
    xq[0, :, :, 48:TP] = hT[:, :, 0:528]
    for q in range(1, 4):
        g0 = NMETA + 512 * q
        xq[q, :, :, 64:TP] = hT[:, :, g0:g0 + 512]
    return xq.reshape(4, 128, 16 * TP)


_NC_CACHE = {}


def kernel(**inputs):
    inp = {k: np.asarray(v, dtype=np.float32) for k, v in inputs.items()}
    if "nc" not in _NC_CACHE:
        _NC_CACHE["nc"] = build_program()
    nc = _NC_CACHE["nc"]
    lay = [prep_layer(i, inp) for i in range(L)]
    wblk = np.stack([a[0] for a in lay])
    wdt = np.stack([a[1] for a in lay])
    par = np.stack([a[2] for a in lay])
    par2 = np.stack([a[3] for a in lay])
    in_maps = []
    for c in range(8):
        b = c % 4
        in_maps.append({"xq": prep_x(inp["x"][b], inp["meta_tokens"]), "par": par, "par2": par2,
                        "wblk": wblk, "wdt": wdt})
    res = run_bass_kernel_spmd(nc, in_maps, core_ids=list(range(8)))
    out = np.zeros((4, SEQ, D), np.float32)
    for b in range(4):
        yq = np.asarray(res.results[b]["yq"]).reshape(4, 128, 16, 512)
        out[b] = yq.transpose(0, 3, 2, 1).reshape(SEQ, D)
    return out
```

```python
import contextlib
import numpy as np
import concourse.bass as bass
import concourse.mybir as mybir
from concourse.bass_utils import run_bass_kernel_spmd

F32 = mybir.dt.float32
BF16 = mybir.dt.bfloat16
AF = mybir.ActivationFunctionType
ALU = mybir.AluOpType

D = 2048
L = 2
SEQ = 2048
NMETA = 16
LT = SEQ + NMETA
TP = 576
EPS = 1e-5
ALPHA = (2.0 * L) ** 0.25
NBLK = 94
BLK = 8192


class Res:
    __slots__ = ("name", "w", "r", "excl")

    def __init__(self, name="", excl=False):
        self.name = name
        self.w = None
        self.r = []
        self.excl = excl


class Prog:
    ENGS = ("pe", "act", "dve", "pool", "sp")
    EPOCH = 30000

    def __init__(self, nc, n_chan=6, same_engine_sync=True):
        self.nc = nc
        self.ops = {e: [] for e in self.ENGS}
        self.cnt = {e: 0 for e in self.ENGS}
        self.seen = {e: {} for e in self.ENGS}
        self.same_engine_sync = same_engine_sync
        self.n_chan = n_chan
        self.chan_total = {}
        self.chan_rr = {e: 0 for e in self.ENGS}
        self.sems = {}
        self.last = {e: None for e in self.ENGS}

    def _deps(self, eng, reads, writes):
        deps = {}

        def add(d):
            if d is None:
                return
            k, v = d
            if deps.get(k, 0) < v:
                deps[k] = v
        for r in reads:
            add(r.w)
        for w in writes:
            add(w.w)
            for d in w.r:
                add(d)
        out = []
        for k, v in deps.items():
            if k[0] == eng and k[1] != "ch" and (eng == "pe" or not self.same_engine_sync):
                continue
            if self.seen[eng].get(k, 0) >= v:
                continue
            self.seen[eng][k] = v
            out.append((k, v))
        return out

    def _mark(self, done, reads, writes):
        for r in reads:
            r.r.append(done)
            if len(r.r) > 64:
                best = {}
                for k, v in r.r:
                    if best.get(k, 0) < v:
                        best[k] = v
                r.r = list(best.items())
        for w in writes:
            w.w = done
            w.r = []

    def op(self, eng, name, *args, r=(), w=(), **kw):
        reads, writes = list(r), list(w)
        writes += [x for x in reads if x.excl]
        reads = [x for x in reads if not x.excl]
        fn = (name, args, kw)
        waits = self._deps(eng, reads, writes)
        ep, idx = divmod(self.cnt[eng], self.EPOCH)
        self.cnt[eng] += 1
        done = ((eng, ep), idx + 1)
        self.last[eng] = done
        self.ops[eng].append(("op", fn, waits, (eng, ep)))
        self._mark(done, reads, writes)

    def dma(self, eng, out, in_, r=(), w=(), chan=None):
        reads, writes = list(r), list(w)
        fn = ("dma_start", (), {"out": out, "in_": in_})
        if chan is None:
            chan = (eng, "ch", self.chan_rr[eng] % self.n_chan)
            self.chan_rr[eng] += 1
        prev = self.chan_total.get(chan, 0)
        waits = self._deps(eng, reads, writes)
        if prev and self.seen[eng].get(chan, 0) < prev:
            self.seen[eng][chan] = prev
            waits.append((chan, prev))
        tot = prev + 16
        self.chan_total[chan] = tot
        self.ops[eng].append(("dma", fn, waits, chan))
        self._mark((chan, tot), reads, writes)

    def barrier(self):
        for e in self.ENGS:
            waits = []
            for k0 in self.ENGS:
                if k0 == e or self.last[k0] is None:
                    continue
                k, v = self.last[k0]
                if self.seen[e].get(k, 0) < v:
                    self.seen[e][k] = v
                    waits.append((k, v))
            for k, v in self.chan_total.items():
                if self.seen[e].get(k, 0) < v:
                    self.seen[e][k] = v
                    waits.append((k, v))
            if waits:
                self.ops[e].append(("wait", None, waits, None))

    def emit(self):
        nc = self.nc
        self.barrier()
        keys = []
        for e in self.ENGS:
            for ep in range((max(self.cnt[e], 1) - 1) // self.EPOCH + 1):
                keys.append((e, ep))
        keys += list(self.chan_total.keys())
        with contextlib.ExitStack() as st:
            for k in keys:
                nm = "s_" + "_".join(str(x) for x in k)
                self.sems[k] = st.enter_context(nc.semaphore(nm))
            block = st.enter_context(nc.Block())
            sems = self.sems

            def run(ename):
                def body(e):
                    for kind, fn, waits, chan in self.ops[ename]:
                        for k, v in waits:
                            e.wait_ge(sems[k], v)
                        if kind == "op":
                            getattr(e, fn[0])(*fn[1], **fn[2]).then_inc(sems[chan], 1)
                        elif kind == "dma":
                            getattr(e, fn[0])(*fn[1], **fn[2]).then_inc(sems[chan], 16)
                return body

            block.tensor(run("pe"))
            block.scalar(run("act"))
            block.vector(run("dve"))
            block.gpsimd(run("pool"))
            block.sync(run("sp"))


PO = {}
_o = 0
for _n, _w in (("cawk", 16 * 31), ("cab", 16), ("lag", 16), ("lab", 16), ("cbw", 48 * 4), ("cbb", 48),
               ("cfw", 88 * 3), ("cfb", 88), ("l1g", 16), ("l1b", 16), ("l2g", 16), ("l2b", 16),
               ("ling", 16), ("linb", 16), ("nbg", 32)):
    PO[_n] = _o
    _o += _w
NPAR = _o
P2 = {"dtb": 0, "alog": 64, "dsk": 128}
NP2 = 192
NS = 2
USE_WCACHE = False


def build_program(n_layers=L, n_quarters=4, dbg=None):
    nc = bass.Bass("TRN2", target_bir_lowering=False)
    xq = nc.dram_tensor("xq", [4, 128, 16 * TP], F32, kind="ExternalInput").ap()
    par_d = nc.dram_tensor("par", [L, 128, NPAR], F32, kind="ExternalInput").ap()
    par2_d = nc.dram_tensor("par2", [L, 64, NP2], F32, kind="ExternalInput").ap()
    wblk_d = nc.dram_tensor("wblk", [L, NBLK, 128, BLK], F32, kind="ExternalInput").ap()
    wdt_d = nc.dram_tensor("wdt", [L, 128, 16 * 64], F32, kind="ExternalInput").ap()
    yq = nc.dram_tensor("yq", [4, 128, 16 * 512], F32, kind="ExternalOutput").ap()
    hscr = nc.dram_tensor("hscr", [4, 128, 16 * TP], F32).ap()
    csscr = nc.dram_tensor("csscr", [9, 64, 64], F32).ap()
    sscr = nc.dram_tensor("sscr", [8, 128, 512], F32).ap()
    hpark = nc.dram_tensor("hpark", [128, 16 * TP], F32).ap()
    wcache = [nc.dram_tensor(f"wcache{i}", [NBLK, 128, BLK], BF16).ap() for i in range(L)]
    dbg_d = None
    if dbg is not None:
        dbg_d = nc.dram_tensor("dbg", [128, 16, TP], F32, kind="ExternalOutput").ap()

    P = Prog(nc)
    O = P.op
    with contextlib.ExitStack() as st:
        def sb(name, shape, dt):
            return st.enter_context(nc.sbuf_tensor(name, shape, dt))

        def RL(n, k):
            return [Res(f"{n}{i}") for i in range(k)]

        h = sb("h", [128, 16, TP], F32); h_r = RL("h", 16)
        hb = sb("hb", [128, 16, TP], BF16); hb_r = RL("hb", 16)
        wring = [sb(f"wr{i}", [128, BLK], BF16) for i in range(NS)]; wring_r = RL("wr", NS + 1)
        wring.append(h[:].rearrange("p a t -> p (a t)")[:, 4800:8896].bitcast(BF16))
        abuf = sb("abuf", [128, 16, TP], BF16); abuf_r = RL("abuf", 16)
        yn = sb("yn", [128, 32, TP], BF16); yn_r = RL("yn", 32)
        cab, cab_r = yn, yn_r
        mbuf, m_r = abuf, abuf_r
        S1 = sb("S1", [128, 512], F32); S_r = Res("S")
        Sb1 = sb("Sb1", [128, 512], BF16); Sb_r = Res("Sb")
        halo_a = sb("halo_a", [128, 16, 30], BF16); halo_a_r = RL("ha", 16)
        halo_b = sb("halo_b", [128, 48, 3], BF16); halo_b_r = RL("hbh", 48)
        halo_f = sb("halo_f", [128, 88, 2], BF16); halo_f_r = RL("hf", 88)
        par = sb("par_sb", [128, NPAR], F32); par_r = Res("par")
        par2 = sb("par2_sb", [64, NP2], F32); par2_r = Res("par2")
        parh = sb("parh", [128, 48], F32); parh_r = Res("parh")
        wdt = sb("wdt_sb", [128, 16, 64], BF16); wdt_r = Res("wdt")
        ones_bf = sb("ones_bf", [128, 128], BF16)
        ones_f = sb("ones_f", [64, 128], F32)
        identf = sb("identf", [128, 128], F32)
        ident = sb("ident", [128, 128], BF16)
        triu = sb("triu", [64, 64], F32)
        pmask = sb("pmask", [64, 1], F32)
        const_r = Res("const")
        dt_tok = sb("dt_tok", [64, 9, 64], F32); dt_r = Res("dt")
        cs_tok = sb("cs_tok", [64, 9, 64], F32); cs_r = Res("cs")
        dfs = sb("dfs", [64, 9, 64], F32); dfs_r = Res("dfs")
        dte = sb("dte", [64, 9, 64], F32); dte_r = Res("dte")
        cdec = sb("cdec", [128, 9, 64], F32); cdec_r = Res("cdec")
        scr46 = sb("scr46", [128, 1152], F32)
        cs_fm = scr46[0:64, 0:576].rearrange("p (c h) -> p c h", c=9); csfm_r = Res("csfm")
        sz_fm = scr46[:, :].bitcast(BF16).rearrange("p (c t) -> p c t", c=4); sz_r = RL("sz", 4)
        da_tok, da_r = cs_fm, csfm_r
        a_rep = sb("a_rep", [64, 64], F32); arep_r = Res("arep")
        csbc = [sb(f"csbc{i}", [64, 512], F32) for i in range(2)]; csbc_r = RL("csbc", 2)
        xs_fm = sb("xs_fm", [128, 4, TP], BF16); xs_r = RL("xs", 4)
        B_fm = sb("B_fm", [128, TP], BF16); B_r = Res("B")
        C_fm = sb("C_fm", [128, TP], BF16); C_r = Res("C")
        xin = [sb(f"xin{i}", [128, 3 + TP], BF16) for i in range(2)]; xin_r = RL("xin", 2)
        xs_tok = sb("xs_tok", [64, 512], BF16); xst_r = Res("xst")
        B_tok = sb("B_tok", [64, 128], BF16); bt_r = Res("bt")
        CBTm = sb("CBTm", [64, 64], F32); cbm_r = Res("cbm")

        MT = sb("MT", [64, 512], BF16); mt_r = Res("mt")
        xdt = sb("xdt", [64, 512], BF16); xdt_r = Res("xdt")
        xdte = sb("xdte", [64, 512], BF16); xdte_r = Res("xdte")
        dxs = sb("dxs", [64, 512], BF16); dxs_r = Res("dxs")
        yt = sb("yt", [64, 512], F32); yt_r = Res("yt")
        ynt = sb("ynt", [64, 512], BF16); ynt_r = Res("ynt")
        ss = sb("ss", [64, 2], F32); ss_r = Res("ss")

        tmpf = [sb(f"tmpf{i}", [128, 512], F32) for i in range(3)]; tmpf_r = RL("tmpf", 3)
        tmpb = [sb(f"tmpb{i}", [128, 512], BF16) for i in range(2)]; tmpb_r = RL("tmpb", 2)
        mean = sb("mean", [128, 512], F32); mean_r = Res("mean")
        rstd = sb("rstd", [128, 512], F32); rstd_r = Res("rstd")
        t2, t2_r = mean, mean_r
        diff, diff_r = rstd, rstd_r
        diag = [sb(f"diag{i}", [128, 128], BF16) for i in range(8)]; diag_r = RL("diag", 8)
        fin, fin_r = xin, xin_r
        pb = [st.enter_context(nc.psum_tensor(f"pb{i}", [128, 512], F32)) if i not in (3, 5) else None for i in range(8)]
        pbb = st.enter_context(nc.psum_tensor("pbb", [128, 1024], BF16))
        pbc = st.enter_context(nc.psum_tensor("pbc", [128, 1024], BF16))
        pb_r = [Res(f"pb{i}", excl=True) for i in range(8)]
        print("sbuf bytes remaining", nc.sbuf_bytes_remaining)

        cnt = {"tf": 0, "tb": 0, "dg": 0, "xin": 0, "fin": 0, "ck": 0}

        def nxt(key, n):
            i = cnt[key] % n
            cnt[key] += 1
            return i

        def init_consts():
            O("pool", "memset", ones_bf[:], 1.0, w=[const_r])
            O("pool", "memset", ones_f[:], 1.0, w=[const_r])
            O("pool", "memset", identf[:], 1.0, w=[const_r])
            O("pool", "affine_select", out=identf[:], in_=identf[:], pattern=[[-1, 128]], compare_op=ALU.is_equal,
              fill=0.0, base=0, channel_multiplier=1, r=[const_r], w=[const_r])
            O("pool", "tensor_copy", out=ident[:], in_=identf[:], r=[const_r], w=[const_r])
            O("pool", "memset", triu[:], 1.0, w=[const_r])
            O("pool", "affine_select", out=triu[:], in_=triu[:], pattern=[[1, 64]], compare_op=ALU.is_ge,
              fill=0.0, base=0, channel_multiplier=-1, r=[const_r], w=[const_r])
            O("pool", "memset", pmask[:], 1.0, w=[const_r])
            O("pool", "affine_select", out=pmask[:], in_=pmask[:], pattern=[[0, 1]], compare_op=ALU.is_ge,
              fill=0.0, base=-48, channel_multiplier=1, r=[const_r], w=[const_r])
            O("pool", "memset", hb[:], 0.0, w=hb_r)
            O("pool", "memset", abuf[:], 0.0, w=abuf_r)
            O("pool", "memset", yn[:], 0.0, w=yn_r)
            O("pool", "memset", h[:], 0.0, w=h_r)
            O("pool", "memset", xs_fm[:], 0.0, w=xs_r)
            O("pool", "memset", B_fm[:], 0.0, w=[B_r])
            O("pool", "memset", C_fm[:], 0.0, w=[C_r])
            for i in range(2):
                O("pool", "memset", xin[i][:], 0.0, w=[xin_r[i]])
            O("pool", "memset", ss[:], 0.0, w=[ss_r])

        wstate = {"issued": 0, "list": []}
        wuse = {"i": 0}

        wc_r = {}
        slot_of = {}
        hfree = {"v": False}

        def w_issue(cur):
            while wstate["issued"] < min(cur + 3, len(wstate["list"])):
                j = wstate["issued"]
                l, q, b = wstate["list"][j]
                live = {slot_of[k] for k in range(cur, j)}
                allowed = [0, 1, 2] if (b < 52 and hfree["v"]) else [0, 1]
                free = [x for x in allowed if x not in live]
                if not free:
                    break
                sl = free[0]
                slot_of[j] = sl
                extra = h_r if sl == 2 else []
                P.dma("pool", wring[sl][:] if sl < 2 else wring[sl], wblk_d[l, b, :, :], w=[wring_r[sl]] + extra, chan=("pool", "ch", "w%d" % sl))
                wstate["issued"] += 1

        def w_next():
            i = wuse["i"]
            wuse["i"] += 1
            w_issue(i)
            sl = slot_of[i]
            return (wring[sl] if sl < 2 else wring[sl]), wring_r[sl]

        def pcol(name, idx):
            o = PO[name] + idx
            return par[:, o:o + 1]

        def mm_group(banks, tiles, KC, lhsT_fn, rhs_fn, reads, M=128):
            for ti, (c0, n) in enumerate(tiles):
                bi = banks[ti]
                for k in range(KC):
                    O("pe", "matmul", pb[bi][0:M, 0:n], lhsT_fn(k), rhs_fn(k, c0, n), start=(k == 0), stop=(k == KC - 1),
                      r=reads, w=[pb_r[bi]])

        def layernorm(tiles, src_fn, src_r, KC, stat_banks, emit_out):
            b1, b2 = stat_banks
            inv = 1.0 / (KC * 128)
            for ti, (c0, n) in enumerate(tiles):
                for c in range(KC):
                    i1 = nxt("tb", 2)
                    O("act", "activation", out=tmpb[i1][:, 0:n], in_=src_fn(c, c0, n), func=AF.Copy, r=[src_r[c]], w=[tmpb_r[i1]])
                    O("pe", "matmul", pb[b1][:, 0:n], ones_bf[:], tmpb[i1][:, 0:n], start=(c == 0), stop=(c == KC - 1),
                      r=[tmpb_r[i1], const_r], w=[pb_r[b1]])
                    i2 = nxt("tb", 2)
                    O("dve", "tensor_tensor", out=tmpb[i2][:, 0:n], in0=src_fn(c, c0, n), in1=src_fn(c, c0, n), op=ALU.mult,
                      r=[src_r[c]], w=[tmpb_r[i2]])
                    O("pe", "matmul", pb[b2][:, 0:n], ones_bf[:], tmpb[i2][:, 0:n], start=(c == 0), stop=(c == KC - 1),
                      r=[tmpb_r[i2], const_r], w=[pb_r[b2]])
                O("act", "activation", out=mean[:, 0:n], in_=pb[b1][:, 0:n], func=AF.Copy, scale=inv, r=[pb_r[b1]], w=[mean_r])
                it = nxt("tf", 3)
                O("dve", "tensor_tensor", out=tmpf[it][:, 0:n], in0=mean[:, 0:n], in1=mean[:, 0:n], op=ALU.mult, r=[mean_r], w=[tmpf_r[it]])
                O("dve", "scalar_tensor_tensor", out=rstd[:, 0:n], in0=pb[b2][:, 0:n], scalar=inv, in1=tmpf[it][:, 0:n],
                  op0=ALU.mult, op1=ALU.subtract, r=[pb_r[b2], tmpf_r[it]], w=[rstd_r])
                O("dve", "tensor_scalar_max", out=rstd[:, 0:n], in0=rstd[:, 0:n], scalar1=0.0, r=[rstd_r], w=[rstd_r])
                O("act", "activation", out=rstd[:, 0:n], in_=rstd[:, 0:n], func=AF.Sqrt, bias=EPS, scale=1.0, r=[rstd_r], w=[rstd_r])
                O("dve", "reciprocal", out=rstd[:, 0:n], in_=rstd[:, 0:n], r=[rstd_r], w=[rstd_r])
                for c in range(KC):
                    it = nxt("tf", 3)
                    O("dve", "tensor_tensor", out=tmpf[it][:, 0:n], in0=src_fn(c, c0, n), in1=mean[:, 0:n], op=ALU.subtract,
                      r=[src_r[c], mean_r], w=[tmpf_r[it]])
                    O("dve", "tensor_tensor", out=tmpf[it][:, 0:n], in0=tmpf[it][:, 0:n], in1=rstd[:, 0:n], op=ALU.mult,
                      r=[rstd_r, tmpf_r[it]], w=[tmpf_r[it]])
                    emit_out(c, c0, n, tmpf[it][:, 0:n], tmpf_r[it])

        def conv_mm(bank, K, wname, chunk, src_fn, src_reads, c0, n):
            for k0 in range(0, K, 8):
                kk = list(range(k0, min(K, k0 + 8)))
                ids = []
                for k in kk:
                    di = nxt("dg", 8)
                    o = PO[wname] + chunk * K + k
                    O("dve", "tensor_scalar_mul", out=diag[di][:], in0=ident[:], scalar1=par[:, o:o + 1],
                      r=[const_r, par_r], w=[diag_r[di]])
                    ids.append(di)
                for k, di in zip(kk, ids):
                    O("pe", "matmul", pb[bank][:, 0:n], diag[di][:], src_fn(c0 - (K - 1) + k, n), start=(k == 0), stop=(k == K - 1),
                      r=[diag_r[di]] + src_reads, w=[pb_r[bank]])

        def act_ap(j, c0, n):
            return yn[:, j, c0:c0 + n] if j < 32 else abuf[:, j - 32, c0:c0 + n]

        def act_res(j):
            return yn_r[j] if j < 32 else abuf_r[j - 32]

        sscr_r = RL("sscr", 8)
        hscr_r = RL("hscr", 4)

        def quarter(l, q, last):
            r0 = 48 if q == 0 else 64
            T = TP - r0
            tiles = [(r0, T // 2), (r0 + T // 2, T // 2)] if q == 0 else [(64, 512)]
            nt = len(tiles)
            g0 = 0 if q == 0 else NMETA + 512 * q
            chunks = list(range(0, 9)) if q == 0 else list(range(1, 9))
            hbk = lambda k, c0, n: hb[:, k, c0:c0 + n]
            bsets = [[0], [1], [2], [4]] if nt == 1 else [[0, 1], [2, 4]]
            bctr = {"i": 0}

            def next_banks():
                b = bsets[bctr["i"] % len(bsets)]
                bctr["i"] += 1
                return b

            hflat = h[:].rearrange("p a t -> p (a t)")
            hpark_r = Res("hpark")
            if l == 0:
                hfree["v"] = False
                P.dma("sp", hflat, xq[q, :, :], w=h_r + [wring_r[2]])

                def out_in(c, c0, n, tn, tn_r):
                    O("act", "activation", out=h[:, c, c0:c0 + n], in_=tn, func=AF.Identity, scale=pcol("ling", c), bias=pcol("linb", c),
                      r=[tn_r, par_r], w=[h_r[c]])
                    O("act", "activation", out=hb[:, c, c0:c0 + n], in_=tn, func=AF.Identity, scale=pcol("ling", c), bias=pcol("linb", c),
                      r=[tn_r, par_r], w=[hb_r[c]])
                layernorm(tiles, lambda c, c0, n: h[:, c, c0:c0 + n], h_r, 16, (6, 7), out_in)
                P.dma("sp", hpark[:, :], hflat, r=h_r, w=[hpark_r])
                hfree["v"] = True
                hsrc, hsrc_r = hpark[:, :], hpark_r
            else:
                P.dma("pool", hb[:].rearrange("p a t -> p (a t)"), hscr[q, :, :], r=[hscr_r[q]], w=hb_r)
                hsrc, hsrc_r = hscr[q, :, :], hscr_r[q]
            if dbg is not None and dbg[0] == "h0" and (l, q) == dbg[1]:
                return "dump_h"

            for j in range(8):
                wv, wr = w_next()
                w3 = wv[:, :].rearrange("p (k c) -> p k c", k=16)
                for cc in range(2):
                    c = 2 * j + cc
                    ba = next_banks()
                    bb = next_banks()
                    mm_group(ba, tiles, 16, lambda k: w3[:, k, (2 * cc) * 128:(2 * cc + 1) * 128], hbk, [wr] + hb_r)
                    mm_group(bb, tiles, 16, lambda k: w3[:, k, (2 * cc + 1) * 128:(2 * cc + 2) * 128], hbk, [wr] + hb_r)
                    if q > 0:
                        O("pool", "tensor_copy", out=abuf[:, c, r0 - 30:r0], in_=halo_a[:, c, :], r=[halo_a_r[c]], w=[abuf_r[c]])
                    else:
                        O("pool", "memset", abuf[:, c, 0:r0], 0.0, w=[abuf_r[c]])
                    for ti, (c0, n) in enumerate(tiles):
                        it = nxt("tf", 3)
                        O("act", "activation", out=tmpf[it][:, 0:n], in_=pb[bb[ti]][:, 0:n], func=AF.Sigmoid, r=[pb_r[bb[ti]]], w=[tmpf_r[it]])
                        O("dve", "tensor_tensor", out=abuf[:, c, c0:c0 + n], in0=pb[ba[ti]][:, 0:n], in1=tmpf[it][:, 0:n], op=ALU.mult,
                          r=[pb_r[ba[ti]], tmpf_r[it]], w=[abuf_r[c]])
                    O("pool", "tensor_copy", out=halo_a[:, c, :], in_=abuf[:, c, TP - 30:TP], r=[abuf_r[c]], w=[halo_a_r[c]])
                    for ti, (c0, n) in enumerate(tiles):
                        bk = (6, 7)[cnt["ck"] % 2]
                        cnt["ck"] += 1
                        conv_mm(bk, 31, "cawk", c, lambda off, n_: abuf[:, c, off:off + n_], [abuf_r[c]], c0, n)
                        O("act", "activation", out=cab[:, c, c0:c0 + n], in_=pb[bk][:, 0:n], func=AF.Identity, bias=pcol("cab", c), scale=1.0,
                          r=[pb_r[bk], par_r], w=[cab_r[c]])

            if dbg is not None and dbg[0] == "a" and (l, q) == dbg[1]:
                return "dump_a"
            if dbg is not None and dbg[0] == "ca" and (l, q) == dbg[1]:
                return "dump_yn"

            def out_sa(c, c0, n, tn, tn_r):
                O("act", "activation", out=cab[:, c, c0:c0 + n], in_=tn, func=AF.Silu, scale=pcol("lag", c), bias=pcol("lab", c),
                  r=[tn_r, par_r], w=[cab_r[c]])
            layernorm(tiles, lambda c, c0, n: cab[:, c, c0:c0 + n], cab_r, 16, (6, 7), out_sa)

            def out_and_gate(KC, act_fn, act_reads, first):
                for jg in range(4):
                    gv, gr = w_next()
                    g3 = gv[:, :].rearrange("p (k c) -> p k c", k=16)
                    for cc in range(4):
                        bs = next_banks()
                        mm_group(bs, tiles, 16, lambda k: g3[:, k, cc * 128:(cc + 1) * 128], hbk, [gr] + hb_r)
                        for ti, (c0, n) in enumerate(tiles):
                            O("act", "activation", out=xs_fm[:, cc, c0:c0 + n], in_=pb[bs[ti]][:, 0:n], func=AF.Sigmoid,
                              r=[pb_r[bs[ti]]], w=[xs_r[cc]])
                    nb = 1 if KC == 16 else 2
                    ncol = 512 // nb
                    for b2 in range(nb):
                        wv, wr = w_next()
                        w3 = wv[:, :].rearrange("p (k c) -> p k c", k=KC)
                        for cc2 in range(ncol // 128):
                            cc = b2 * (ncol // 128) + cc2
                            c = jg * 4 + cc
                            bs = next_banks()
                            mm_group(bs, tiles, KC, lambda k: w3[:, k, cc2 * 128:(cc2 + 1) * 128], act_fn, [wr] + act_reads)
                            for ti, (c0, n) in enumerate(tiles):
                                if first:
                                    O("dve", "tensor_tensor", out=mbuf[:, c, c0:c0 + n], in0=pb[bs[ti]][:, 0:n], in1=xs_fm[:, cc, c0:c0 + n], op=ALU.mult,
                                      r=[pb_r[bs[ti]], xs_r[cc]], w=[m_r[c]])
                                else:
                                    it = nxt("tf", 3)
                                    O("dve", "tensor_tensor", out=tmpf[it][:, 0:n], in0=pb[bs[ti]][:, 0:n], in1=xs_fm[:, cc, c0:c0 + n], op=ALU.mult,
                                      r=[pb_r[bs[ti]], xs_r[cc]], w=[tmpf_r[it]])
                                    O("dve", "tensor_tensor", out=mbuf[:, c, c0:c0 + n], in0=tmpf[it][:, 0:n], in1=mbuf[:, c, c0:c0 + n], op=ALU.add,
                                      r=[tmpf_r[it], m_r[c]], w=[m_r[c]])
            def dt_stage():
                for ci in chunks:
                    for k in range(16):
                        O("pe", "matmul", pb[6][0:64, 0:64], hb[:, k, ci * 64:(ci + 1) * 64], wdt[:, k, :], start=(k == 0), stop=(k == 15),
                          r=hb_r + [wdt_r], w=[pb_r[6]])
                    O("dve", "tensor_tensor", out=dt_tok[:, ci, :], in0=pb[6][0:64, 0:64], in1=par2[0:64, P2["dtb"]:P2["dtb"] + 64], op=ALU.add,
                      r=[pb_r[6], par2_r], w=[dt_r])
                cl = slice(chunks[0], 9)
                ncl = 9 - chunks[0]
                O("dve", "tensor_scalar_min", out=dt_tok[:, cl, :], in0=dt_tok[:, cl, :], scalar1=60.0, r=[dt_r], w=[dt_r])
                O("act", "activation", out=dt_tok[:, cl, :], in_=dt_tok[:, cl, :], func=AF.Exp, r=[dt_r], w=[dt_r])
                O("act", "activation", out=dt_tok[:, cl, :], in_=dt_tok[:, cl, :], func=AF.Ln, bias=1.0, scale=1.0, r=[dt_r], w=[dt_r])
                if q == 0:
                    O("dve", "tensor_scalar_mul", out=dt_tok[:, 0, :], in0=dt_tok[:, 0, :], scalar1=pmask[:, 0:1], r=[dt_r, const_r], w=[dt_r])
                O("dve", "tensor_tensor", out=da_tok[:, cl, :], in0=dt_tok[:, cl, :], in1=a_rep[:, :].unsqueeze(1).to_broadcast([64, ncl, 64]),
                  op=ALU.mult, r=[dt_r, arep_r], w=[da_r])
                daf = scr46[0:64, 0:576]
                csf = cs_tok[:].rearrange("p c h -> p (c h)")
                cdf = cdec[:].rearrange("p c h -> p (c h)")
                dtf = dte[:].rearrange("p c h -> p (c h)")
                lo = chunks[0] * 64
                for (o, n) in ((lo, 288), (lo + 288, 576 - lo - 288)):
                    O("pe", "matmul", pb[6][0:64, 0:n], triu[:, :], daf[:, o:o + n], start=True, stop=True, r=[da_r, const_r], w=[pb_r[6]])
                    O("pe", "matmul", pb[7][:, 0:n], ones_f[:, :], daf[:, o:o + n], start=True, stop=True, r=[da_r, const_r], w=[pb_r[7]])
                    O("dve", "tensor_copy", out=csf[:, o:o + n], in_=pb[6][0:64, 0:n], r=[pb_r[6]], w=[cs_r])
                    O("act", "activation", out=cdf[:, o:o + n], in_=pb[7][:, 0:n], func=AF.Exp, r=[pb_r[7]], w=[cdec_r])
                    O("dve", "tensor_tensor", out=dtf[:, o:o + n], in0=pb[7][0:64, 0:n], in1=csf[:, o:o + n], op=ALU.subtract,
                      r=[pb_r[7], cs_r], w=[dte_r])
                O("act", "activation", out=dte[:, cl, :], in_=dte[:, cl, :], func=AF.Exp, r=[dte_r], w=[dte_r])
                O("act", "activation", out=dfs[:, cl, :], in_=cs_tok[:, cl, :], func=AF.Exp, r=[cs_r], w=[dfs_r])
                for ci in chunks:
                    O("pe", "transpose", pb[6][0:64, 0:64], cs_tok[:, ci, :], identf[0:64, 0:64], r=[cs_r, const_r, da_r], w=[pb_r[6]])
                    O("dve", "tensor_copy", out=cs_fm[:, ci, :], in_=pb[6][0:64, 0:64], r=[pb_r[6]], w=[csfm_r])
                csscr_r = Res("csscr")
                P.dma("sp", csscr[chunks[0]:9, :, :].rearrange("c h q -> h c q"), cs_fm[:, cl, :], r=[csfm_r], w=[csscr_r])

                return csscr_r

            csscr_r = dt_stage()
            if dbg is not None and dbg[0] == "sa" and (l, q) == dbg[1]:
                return "dump_yn"
            out_and_gate(16, lambda k, c0, n: cab[:, k, c0:c0 + n], cab_r, True)
            if dbg is not None and dbg[0] == "ma" and (l, q) == dbg[1]:
                return "dump_m"

            P.barrier()
            hraw = h[:].rearrange("p a t -> p (a t)")
            xs_fm2 = hraw[:, 0:1152].bitcast(BF16).rearrange("p (c t) -> p c t", c=4)
            sz_fm2 = hraw[:, 1152:2304].bitcast(BF16).rearrange("p (c t) -> p c t", c=4)
            bc2 = hraw[:, 2304:2880].bitcast(BF16)
            B_fm2, C_fm2 = bc2[:, 0:576], bc2[:, 576:1152]
            set2_r = dict(xs=RL("xs2_", 4), sz=RL("sz2_", 4), B=Res("B2"), C=Res("C2"))
            O("pool", "memset", xs_fm2[:, :, 0:64], 0.0, w=set2_r["xs"])
            O("pool", "memset", sz_fm2[:, :, 0:64], 0.0, w=set2_r["sz"])
            O("pool", "memset", B_fm2[:, 0:64], 0.0, w=[set2_r["B"]])
            O("pool", "memset", C_fm2[:, 0:64], 0.0, w=[set2_r["C"]])
            tb = hraw[:, 2880:4800]
            tbb = tb.bitcast(BF16)
            TS = [dict(xs_tok=xs_tok[:, :], B_tok=B_tok[:, :], CBTm=CBTm[:, :], diff=diff[0:64, :], MT=MT[:, :], xdt=xdt[:, :], xdte=xdte[:, :], dxs=dxs[:, :],
                       xst_r=xst_r, bt_r=bt_r, cbm_r=cbm_r, diff_r=diff_r, mt_r=mt_r, xdt_r=xdt_r, xdte_r=xdte_r, dxs_r=dxs_r),
                  dict(xs_tok=tbb[0:64, 0:512], B_tok=tbb[0:64, 512:640], MT=tbb[0:64, 640:1152], xdt=tbb[0:64, 1152:1664],
                       xdte=tbb[0:64, 1664:2176], dxs=tbb[0:64, 2176:2688], CBTm=tb[0:64, 1344:1408], diff=tb[0:64, 1408:1920],
                       xst_r=Res("xst2"), bt_r=Res("bt2"), cbm_r=Res("cbm2"), diff_r=Res("diff2"), mt_r=Res("mt2"), xdt_r=Res("xdt2"),
                       xdte_r=Res("xdte2"), dxs_r=Res("dxs2"))]
            GS = [dict(xs=xs_fm, sz=sz_fm, B=B_fm, C=C_fm, xs_r=xs_r, sz_r=sz_r, B_r=B_r, C_r=C_r),
                  dict(xs=xs_fm2, sz=sz_fm2, B=B_fm2, C=C_fm2, xs_r=set2_r["xs"], sz_r=set2_r["sz"], B_r=set2_r["B"], C_r=set2_r["C"])]
            def stop_at(pt):
                return dbg is not None and dbg[0] == "stop" and dbg[2] == pt and (l, q) == dbg[1]
            if stop_at(1):
                return "dump_h"
            P.barrier()
            def conv_chunk(wa, wres, col, pidx, dst_fn, dst_r):
                ba = [0, 1][:nt]
                mm_group(ba, tiles, 16, lambda k: wa[:, k, col * 128:(col + 1) * 128], hbk, [wres] + hb_r)
                xi = nxt("xin", 2)
                if q > 0:
                    O("pool", "tensor_copy", out=xin[xi][:, r0:r0 + 3], in_=halo_b[:, pidx, :], r=[halo_b_r[pidx]], w=[xin_r[xi]])
                else:
                    O("pool", "memset", xin[xi][:, r0:r0 + 3], 0.0, w=[xin_r[xi]])
                for ti, (c0, n) in enumerate(tiles):
                    O("act", "activation", out=xin[xi][:, 3 + c0:3 + c0 + n], in_=pb[ba[ti]][:, 0:n], func=AF.Copy, r=[pb_r[ba[ti]]], w=[xin_r[xi]])
                O("pool", "tensor_copy", out=halo_b[:, pidx, :], in_=xin[xi][:, TP:TP + 3], r=[xin_r[xi]], w=[halo_b_r[pidx]])
                for ti, (c0, n) in enumerate(tiles):
                    conv_mm(2, 4, "cbw", pidx, lambda off, n_: xin[xi][:, 3 + off:3 + off + n_], [xin_r[xi]], c0, n)
                    i1 = nxt("tf", 3)
                    O("act", "activation", out=tmpf[i1][:, 0:n], in_=pb[2][:, 0:n], func=AF.Tanh, bias=parh[:, pidx:pidx + 1], scale=0.5,
                      r=[pb_r[2], parh_r], w=[tmpf_r[i1]])
                    i2 = nxt("tf", 3)
                    O("dve", "tensor_scalar", out=tmpf[i2][:, 0:n], in0=pb[2][:, 0:n], scalar1=pcol("cbb", pidx), scalar2=0.5, op0=ALU.add, op1=ALU.mult,
                      r=[pb_r[2], par_r], w=[tmpf_r[i2]])
                    O("dve", "scalar_tensor_tensor", out=dst_fn(c0, n), in0=tmpf[i1][:, 0:n], scalar=1.0, in1=tmpf[i2][:, 0:n], op0=ALU.add, op1=ALU.mult,
                      r=[tmpf_r[i1], tmpf_r[i2]], w=[dst_r])

            def prologue_pieces(g):
                G = GS[g % 2]
                wst = {}

                def p_xs(cc):
                    def f():
                        if cc == 0:
                            wv, wr = w_next()
                            wst["w3"] = wv[:, :].rearrange("p (k c) -> p k c", k=16); wst["wr"] = wr
                        conv_chunk(wst["w3"], wst["wr"], cc, g * 4 + cc, lambda c0, n: G["xs"][:, cc, c0:c0 + n], G["xs_r"][cc])
                    return f

                def p_bc(which):
                    def f():
                        if which == 0:
                            wv2, wr2 = w_next()
                            wst["w32"] = wv2[:, 0:4096].rearrange("p (k c) -> p k c", k=16); wst["wr2"] = wr2
                            conv_chunk(wst["w32"], wst["wr2"], 0, 32 + g, lambda c0, n: G["B"][:, c0:c0 + n], G["B_r"])
                        else:
                            conv_chunk(wst["w32"], wst["wr2"], 1, 40 + g, lambda c0, n: G["C"][:, c0:c0 + n], G["C_r"])
                    return f

                def p_z(cc):
                    def f():
                        if cc == 0:
                            wv3, wr3 = w_next()
                            wst["wz"] = wv3[:, :].rearrange("p (k c) -> p k c", k=16); wst["wr3"] = wr3
                        wz, wr3 = wst["wz"], wst["wr3"]
                        bs = [0, 1][:nt]
                        mm_group(bs, tiles, 16, lambda k: wz[:, k, cc * 128:(cc + 1) * 128], hbk, [wr3] + hb_r)
                        for ti, (c0, n) in enumerate(tiles):
                            i1 = nxt("tf", 3)
                            O("act", "activation", out=tmpf[i1][:, 0:n], in_=pb[bs[ti]][:, 0:n], func=AF.Tanh, scale=0.5, r=[pb_r[bs[ti]]], w=[tmpf_r[i1]])
                            O("dve", "scalar_tensor_tensor", out=G["sz"][:, cc, c0:c0 + n], in0=tmpf[i1][:, 0:n], scalar=1.0, in1=pb[bs[ti]][:, 0:n],
                              op0=ALU.add, op1=ALU.mult, r=[tmpf_r[i1], pb_r[bs[ti]]], w=[G["sz_r"][cc]])
                    return f
                return [p_xs(0), p_xs(1), p_xs(2), p_xs(3), p_bc(0), p_bc(1), p_z(0), p_z(1), p_z(2), p_z(3)]

            pend = prologue_pieces(0)
            for g in range(8):
                for f in pend:
                    f()
                pend = prologue_pieces(g + 1) if g < 7 else []
                per = -(-len(pend) // len(chunks)) if pend else 0
                G = GS[g % 2]
                xs_fmG, sz_fmG, B_fmG, C_fmG = G["xs"], G["sz"], G["B"], G["C"]
                xsG_r, szG_r, BG_r, CG_r = G["xs_r"], G["sz_r"], G["B_r"], G["C_r"]
                hs = slice(g * 8, g * 8 + 8)
                if q == 0:
                    O("pool", "memset", S1[:, :], 0.0, w=[S_r])
                else:
                    P.dma("sp", S1[:, :], sscr[g, :, :], r=[sscr_r[g]], w=[S_r])
                O("act", "activation", out=Sb1[:, :], in_=S1[:, :], func=AF.Copy, r=[S_r], w=[Sb_r])
                if stop_at(2):
                    return "dump_h"
                xsT = pbb[0:64, 0:640]
                ynT = pbb[:, 512:1024]
                szT = pbc[0:64, 0:512]

                def stage_a(ci):
                    T = TS[ci % 2]
                    tk = slice(ci * 64, ci * 64 + 64)
                    cb = ci % 2
                    P.dma("sp", csbc[cb][:, :], csscr[ci, g * 8:g * 8 + 8, :].rearrange("h q -> (h q)").partition_broadcast(64),
                          r=[csscr_r], w=[csbc_r[cb]])
                    for cc in range(4):
                        O("pe", "transpose", xsT[:, cc * 128:(cc + 1) * 128], xs_fmG[:, cc, tk], ident[:, :], r=[xsG_r[cc], const_r], w=[pb_r[5]])
                    O("pe", "transpose", xsT[:, 512:640], B_fmG[:, tk], ident[:, :], r=[BG_r, const_r], w=[pb_r[5]])
                    O("act", "activation", out=T["xs_tok"], in_=xsT[:, 0:512], func=AF.Copy, r=[pb_r[5]], w=[T["xst_r"]])
                    O("act", "activation", out=T["B_tok"], in_=xsT[:, 512:640], func=AF.Copy, r=[pb_r[5]], w=[T["bt_r"]])
                    O("pe", "matmul", pb[4][0:64, 0:64], B_fmG[:, tk], C_fmG[:, tk], start=True, stop=True, r=[BG_r, CG_r], w=[pb_r[4]])
                    O("dve", "tensor_tensor", out=T["CBTm"], in0=pb[4][0:64, 0:64], in1=triu[:, :], op=ALU.mult, r=[pb_r[4], const_r], w=[T["cbm_r"]])
                    d3 = T["diff"].rearrange("p (h q) -> p h q", h=8)
                    O("pool", "tensor_tensor", out=d3, in0=csbc[cb][:].rearrange("p (h q) -> p h q", h=8),
                      in1=cs_tok[:, ci, hs].unsqueeze(2).to_broadcast([64, 8, 64]), op=ALU.subtract, r=[csbc_r[cb], cs_r], w=[T["diff_r"]])
                    O("act", "activation", out=T["diff"], in_=T["diff"], func=AF.Exp, r=[T["diff_r"]], w=[T["diff_r"]])
                    O("dve", "scalar_tensor_tensor", out=T["MT"].rearrange("p (h q) -> p h q", h=8), in0=d3, scalar=1.0,
                      in1=T["CBTm"].unsqueeze(1).to_broadcast([64, 8, 64]), op0=ALU.min, op1=ALU.mult, r=[T["diff_r"], T["cbm_r"]], w=[T["mt_r"]])
                    x3 = T["xs_tok"].rearrange("p (h q) -> p h q", h=8)
                    O("pool", "tensor_tensor", out=T["xdt"].rearrange("p (h q) -> p h q", h=8), in0=x3,
                      in1=dt_tok[:, ci, hs].unsqueeze(2).to_broadcast([64, 8, 64]), op=ALU.mult, r=[T["xst_r"], dt_r], w=[T["xdt_r"]])
                    O("pool", "tensor_tensor", out=T["xdte"].rearrange("p (h q) -> p h q", h=8), in0=T["xdt"].rearrange("p (h q) -> p h q", h=8),
                      in1=dte[:, ci, hs].unsqueeze(2).to_broadcast([64, 8, 64]), op=ALU.mult, r=[T["xdt_r"], dte_r], w=[T["xdte_r"]])
                    O("pool", "tensor_tensor", out=T["dxs"].rearrange("p (h q) -> p h q", h=8), in0=x3,
                      in1=par2[0:64, P2["dsk"] + g * 8:P2["dsk"] + g * 8 + 8].unsqueeze(2).to_broadcast([64, 8, 64]), op=ALU.mult,
                      r=[T["xst_r"], par2_r], w=[T["dxs_r"]])

                def stage_b(ci):
                    T = TS[ci % 2]
                    tk = slice(ci * 64, ci * 64 + 64)
                    for r in range(8):
                        rs = slice(r * 64, (r + 1) * 64)
                        O("pe", "matmul", pb[6][0:64, rs], T["MT"][:, rs], T["xdt"][:, rs], start=True, stop=False, r=[T["mt_r"], T["xdt_r"]], w=[pb_r[6]])
                        O("pe", "matmul", pb[6][0:64, rs], ident[0:64, 0:64], T["dxs"][:, rs], start=False, stop=True, r=[T["dxs_r"], const_r], w=[pb_r[6]])
                    O("pe", "matmul", pb[7][0:64, 0:512], C_fmG[:, tk], Sb1[:, :], start=True, stop=True, r=[CG_r, Sb_r], w=[pb_r[7]])
                    y3 = yt[:].rearrange("p (h q) -> p h q", h=8)
                    O("dve", "tensor_tensor", out=y3, in0=pb[7][0:64, 0:512].rearrange("p (h q) -> p h q", h=8),
                      in1=dfs[:, ci, hs].unsqueeze(2).to_broadcast([64, 8, 64]), op=ALU.mult, r=[pb_r[7], dfs_r], w=[yt_r])
                    O("dve", "tensor_tensor", out=yt[:, :], in0=yt[:, :], in1=pb[6][0:64, 0:512], op=ALU.add, r=[pb_r[6], yt_r], w=[yt_r])
                    O("pe", "matmul", pb[1][:, 0:512], T["B_tok"], T["xdte"], start=True, stop=True, r=[T["bt_r"], T["xdte_r"]], w=[pb_r[1]])
                    O("pool", "tensor_tensor", out=t2[:].rearrange("p (h q) -> p h q", h=8), in0=S1[:].rearrange("p (h q) -> p h q", h=8),
                      in1=cdec[:, ci, hs].unsqueeze(2).to_broadcast([128, 8, 64]), op=ALU.mult, r=[S_r, cdec_r], w=[t2_r])
                    O("dve", "tensor_tensor", out=S1[:, :], in0=t2[:, :], in1=pb[1][:, 0:512], op=ALU.add, r=[t2_r, pb_r[1]], w=[S_r])
                    O("act", "activation", out=Sb1[:, :], in_=S1[:, :], func=AF.Copy, r=[S_r], w=[Sb_r])

                def stage_c(ci):
                    tk = slice(ci * 64, ci * 64 + 64)
                    for cc in range(4):
                        O("pe", "transpose", szT[:, cc * 128:(cc + 1) * 128], sz_fmG[:, cc, tk], ident[:, :], r=[szG_r[cc], const_r], w=[pb_r[3]])
                    O("dve", "tensor_tensor", out=ynt[:, :], in0=yt[:, :], in1=szT[:, :], op=ALU.mult, r=[pb_r[3], yt_r], w=[ynt_r])

                def stage_c2(ci):
                    tk = slice(ci * 64, ci * 64 + 64)
                    for cc in range(4):
                        O("pe", "transpose", ynT[:, 128 + cc * 64:128 + (cc + 1) * 64], ynt[:, cc * 128:(cc + 1) * 128], ident[0:64, 0:64],
                          r=[ynt_r, const_r], w=[pb_r[5]])
                    O("act", "activation", out=yn[:, g * 4:g * 4 + 4, tk], in_=ynT[:, 128:384].rearrange("p (c t) -> p c t", c=4), func=AF.Copy,
                      r=[pb_r[5]], w=yn_r[g * 4:g * 4 + 4])

                def group_norm():
                    for ti, (c0, n) in enumerate(tiles):
                        for cc in range(4):
                            i2 = nxt("tb", 2)
                            O("dve", "tensor_tensor", out=tmpb[i2][:, 0:n], in0=yn[:, g * 4 + cc, c0:c0 + n], in1=yn[:, g * 4 + cc, c0:c0 + n], op=ALU.mult,
                              r=[yn_r[g * 4 + cc]], w=[tmpb_r[i2]])
                            O("pe", "matmul", pb[2][:, 0:n], ones_bf[:], tmpb[i2][:, 0:n], start=(cc == 0), stop=(cc == 3),
                              r=[tmpb_r[i2], const_r], w=[pb_r[2]])
                        it = nxt("tf", 3)
                        O("act", "activation", out=tmpf[it][:, 0:n], in_=pb[2][:, 0:n], func=AF.Sqrt, bias=4.0 * EPS, scale=1.0 / 512.0, r=[pb_r[2]], w=[tmpf_r[it]])
                        O("dve", "reciprocal", out=tmpf[it][:, 0:n], in_=tmpf[it][:, 0:n], r=[tmpf_r[it]], w=[tmpf_r[it]])
                        for cc in range(4):
                            o = PO["nbg"] + g * 4 + cc
                            O("dve", "scalar_tensor_tensor", out=yn[:, g * 4 + cc, c0:c0 + n], in0=yn[:, g * 4 + cc, c0:c0 + n], scalar=par[:, o:o + 1],
                              in1=tmpf[it][:, 0:n], op0=ALU.mult, op1=ALU.mult, r=[yn_r[g * 4 + cc], tmpf_r[it], par_r], w=[yn_r[g * 4 + cc]])

                stage_a(chunks[0])
                npc, nch, done = len(pend), len(chunks), 0
                for idx, ci in enumerate(chunks):
                    if idx + 1 < nch:
                        stage_a(chunks[idx + 1])
                    stage_b(ci)
                    if idx > 0:
                        stage_c2(chunks[idx - 1])
                    want = ((idx + 1) * npc + nch - 1) // nch
                    while done < want and pend:
                        pend.pop(0)()
                        done += 1
                    stage_c(ci)
                stage_c2(chunks[-1])
                group_norm()
                P.dma("sp", sscr[g, :, :], S1[:, :], r=[S_r], w=[sscr_r[g]])
            if dbg is not None and dbg[0] == "yn" and (l, q) == dbg[1]:
                return "dump_yn"

            P.barrier()
            out_and_gate(32, lambda k, c0, n: yn[:, k, c0:c0 + n], yn_r, False)

            hfree["v"] = False
            P.dma("sp", hflat, hsrc, r=[hsrc_r], w=h_r + [wring_r[2]])
            for jb in range(4):
                wv, wr = w_next()
                w3 = wv[:, :].rearrange("p (k c) -> p k c", k=16)
                for cc in range(4):
                    c = jb * 4 + cc
                    ba = next_banks()
                    mm_group(ba, tiles, 16, lambda k: w3[:, k, cc * 128:(cc + 1) * 128], lambda k, c0, n: mbuf[:, k, c0:c0 + n], [wr] + m_r)
                    for ti, (c0, n) in enumerate(tiles):
                        O("dve", "scalar_tensor_tensor", out=h[:, c, c0:c0 + n], in0=h[:, c, c0:c0 + n], scalar=ALPHA, in1=pb[ba[ti]][:, 0:n],
                          op0=ALU.mult, op1=ALU.add, r=[pb_r[ba[ti]], h_r[c]], w=[h_r[c]])

            def out_ln(gname, bname, with_hb=False):
                def f(c, c0, n, tn, tn_r):
                    O("act", "activation", out=h[:, c, c0:c0 + n], in_=tn, func=AF.Identity, scale=pcol(gname, c), bias=pcol(bname, c),
                      r=[tn_r, par_r], w=[h_r[c]])
                    if with_hb:
                        O("act", "activation", out=hb[:, c, c0:c0 + n], in_=tn, func=AF.Identity, scale=pcol(gname, c), bias=pcol(bname, c),
                          r=[tn_r, par_r], w=[hb_r[c]])
                return f
            layernorm(tiles, lambda c, c0, n: h[:, c, c0:c0 + n], h_r, 16, (6, 7), out_ln("l1g", "l1b", True))
            if dbg is not None and dbg[0] == "h1" and (l, q) == dbg[1]:
                return "dump_h"

            P.barrier()
            for jb in range(22):
                wv, wr = w_next()
                w3 = wv[:, :].rearrange("p (k c) -> p k c", k=16)
                for cc in range(2):
                    j = 2 * jb + cc
                    outs = []
                    for half in range(2):
                        pidx = j + 44 * half
                        ba = next_banks()
                        mm_group(ba, tiles, 16, lambda k: w3[:, k, (2 * cc + half) * 128:(2 * cc + half + 1) * 128], hbk, [wr] + hb_r)
                        fi = half
                        if q > 0:
                            O("pool", "tensor_copy", out=fin[fi][:, r0:r0 + 2], in_=halo_f[:, pidx, :], r=[halo_f_r[pidx]], w=[fin_r[fi]])
                        else:
                            O("pool", "memset", fin[fi][:, r0:r0 + 2], 0.0, w=[fin_r[fi]])
                        for ti, (c0, n) in enumerate(tiles):
                            O("act", "activation", out=fin[fi][:, 2 + c0:2 + c0 + n], in_=pb[ba[ti]][:, 0:n], func=AF.Copy, r=[pb_r[ba[ti]]], w=[fin_r[fi]])
                        O("pool", "tensor_copy", out=halo_f[:, pidx, :], in_=fin[fi][:, TP:TP + 2], r=[fin_r[fi]], w=[halo_f_r[pidx]])
                        outs.append(fi)
                    for ti, (c0, n) in enumerate(tiles):
                        gi = nxt("tf", 3)
                        conv_mm(6, 3, "cfw", j, lambda off, n_: fin[0][:, 2 + off:2 + off + n_], [fin_r[0]], c0, n)
                        O("act", "activation", out=tmpf[gi][:, 0:n], in_=pb[6][:, 0:n], func=AF.Silu, bias=pcol("cfb", j), scale=1.0,
                          r=[pb_r[6], par_r], w=[tmpf_r[gi]])
                        conv_mm(7, 3, "cfw", j + 44, lambda off, n_: fin[1][:, 2 + off:2 + off + n_], [fin_r[1]], c0, n)
                        O("dve", "scalar_tensor_tensor", out=act_ap(j, c0, n), in0=pb[7][:, 0:n], scalar=pcol("cfb", j + 44), in1=tmpf[gi][:, 0:n],
                          op0=ALU.add, op1=ALU.mult, r=[pb_r[7], tmpf_r[gi], par_r], w=[act_res(j)])
            act_reads = yn_r + abuf_r[0:12]
            for c in range(16):
                wv, wr = w_next()
                w3 = wv[:, 0:44 * 128].rearrange("p (k c) -> p k c", k=44)
                ba = next_banks()
                mm_group(ba, tiles, 44, lambda k: w3[:, k, :], lambda k, c0, n: act_ap(k, c0, n), [wr] + act_reads)
                for ti, (c0, n) in enumerate(tiles):
                    O("dve", "scalar_tensor_tensor", out=h[:, c, c0:c0 + n], in0=h[:, c, c0:c0 + n], scalar=ALPHA, in1=pb[ba[ti]][:, 0:n],
                      op0=ALU.mult, op1=ALU.add, r=[pb_r[ba[ti]], h_r[c]], w=[h_r[c]])
            layernorm(tiles, lambda c, c0, n: h[:, c, c0:c0 + n], h_r, 16, (6, 7), out_ln("l2g", "l2b"))
            if last:
                P.dma("sp", yq[q, :, :].rearrange("p (c t) -> p c t", c=16), h[:, :, 64:TP], r=h_r)
                hfree["v"] = True
            else:
                O("pool", "memset", h[:, :, 0:r0], 0.0, r=h_r, w=h_r)
                P.dma("sp", hscr[q, :, :], hflat, r=h_r, w=[hscr_r[q]])
                hfree["v"] = True
            return None

        init_consts()
        for l in range(n_layers):
            for q in range(n_quarters):
                for b in range(NBLK):
                    wstate["list"].append((l, q, b))
        stop = False
        for l in range(n_layers):
            if stop:
                break
            P.barrier()
            P.dma("sp", par[:, :], par_d[l, :, :], w=[par_r])
            P.dma("sp", par2[:, :], par2_d[l, :, :], w=[par2_r])
            P.dma("pool", wdt[:].rearrange("p k c -> p (k c)"), wdt_d[l, :, :], w=[wdt_r])
            O("dve", "tensor_scalar_mul", out=parh[:, :], in0=par[:, PO["cbb"]:PO["cbb"] + 48], scalar1=0.5, r=[par_r], w=[parh_r])
            O("act", "activation", out=a_rep[:, :], in_=par2[0:64, P2["alog"]:P2["alog"] + 64], func=AF.Exp, r=[par2_r], w=[arep_r])
            O("dve", "tensor_scalar_mul", out=a_rep[:, :], in0=a_rep[:, :], scalar1=-1.0, r=[arep_r], w=[arep_r])
            for q in range(n_quarters):
                rc = quarter(l, q, last=(l == n_layers - 1))
                if rc is not None:
                    P.barrier()
                    if rc == "dump_a":
                        for c in range(16):
                            O("dve", "tensor_copy", out=h[:, c, :], in_=abuf[:, c, :], r=[abuf_r[c]], w=[h_r[c]])
                            P.dma("sp", dbg_d[:, c, :], h[:, c, :], r=[h_r[c]])
                    elif rc == "dump_h":
                        for c in range(16):
                            P.dma("sp", dbg_d[:, c, :], h[:, c, :], r=[h_r[c]])
                    elif rc == "dump_m":
                        for c in range(16):
                            O("dve", "tensor_copy", out=h[:, c, :], in_=mbuf[:, c, :], r=[m_r[c]], w=[h_r[c]])
                            P.dma("sp", dbg_d[:, c, :], h[:, c, :], r=[h_r[c]])
                    elif rc == "dump_yn":
                        for c in range(16):
                            O("dve", "tensor_copy", out=h[:, c, :], in_=yn[:, dbg[2] + c, :], r=[yn_r[dbg[2] + c]], w=[h_r[c]])
                            P.dma("sp", dbg_d[:, c, :], h[:, c, :], r=[h_r[c]])
                    stop = True
                    break
        P.emit()
    return nc


def _col(v, n):
    return np.ascontiguousarray(v.reshape(n, 128).T)


def _blk(w, KC, cols):
    sub = w[:, cols]
    t = sub.reshape(KC, 128, len(cols)).transpose(1, 0, 2).reshape(128, KC * len(cols))
    out = np.zeros((128, BLK), np.float32)
    out[:, :t.shape[1]] = t
    return out


def prep_layer(i, inp):
    w_in = inp["w_in"][i]
    ar = np.arange
    blocks = []
    A1, A2, Z0, X0, DT0, G0 = 0, 2048, 4096, 8192, 14336, 14400
    for j in range(8):
        cols = np.concatenate([A1 + (2 * j) * 128 + ar(128), A2 + (2 * j) * 128 + ar(128),
                               A1 + (2 * j + 1) * 128 + ar(128), A2 + (2 * j + 1) * 128 + ar(128)])
        blocks.append(_blk(w_in, 16, cols))
    for jb in range(4):
        blocks.append(_blk(w_in, 16, G0 + jb * 512 + ar(512)))
        blocks.append(_blk(inp["w_a_out"][i], 16, jb * 512 + ar(512)))
    for g in range(8):
        blocks.append(_blk(w_in, 16, X0 + g * 512 + ar(512)))
        cols = np.concatenate([X0 + 4096 + g * 128 + ar(128), X0 + 4096 + 1024 + g * 128 + ar(128)])
        blocks.append(_blk(w_in, 16, cols))
        blocks.append(_blk(w_in, 16, Z0 + g * 512 + ar(512)))
    for jb in range(8):
        if jb % 2 == 0:
            blocks.append(_blk(w_in, 16, G0 + 2048 + (jb // 2) * 512 + ar(512)))
        blocks.append(_blk(inp["w_b_out"][i], 32, jb * 256 + ar(256)))
    for jb in range(4):
        blocks.append(_blk(inp["w_o"][i], 16, jb * 512 + ar(512)))
    w_up = inp["w_up"][i]
    for jb in range(22):
        cols = np.concatenate([(2 * jb) * 128 + ar(128), 5632 + (2 * jb) * 128 + ar(128),
                               (2 * jb + 1) * 128 + ar(128), 5632 + (2 * jb + 1) * 128 + ar(128)])
        blocks.append(_blk(w_up, 16, cols))
    for c in range(16):
        blocks.append(_blk(inp["w_down"][i], 44, c * 128 + ar(128)))
    assert len(blocks) == NBLK, len(blocks)
    wblk = np.stack(blocks)
    wdt = w_in[:, DT0:DT0 + 64].reshape(16, 128, 64).transpose(1, 0, 2).reshape(128, 16 * 64)
    par = np.zeros((128, NPAR), np.float32)

    def put(name, arr):
        a = arr.reshape(128, -1)
        par[:, PO[name]:PO[name] + a.shape[1]] = a
    put("cawk", inp["conv_a_w"][i].T.reshape(16, 128, 31).transpose(1, 0, 2))
    put("cab", _col(inp["conv_a_b"][i], 16))
    put("lag", _col(inp["ln_a_g"][i], 16))
    put("lab", _col(inp["ln_a_b"][i], 16))
    put("cbw", inp["conv_b_w"][i].T.reshape(48, 128, 4).transpose(1, 0, 2))
    put("cbb", _col(inp["conv_b_b"][i], 48))
    put("cfw", inp["conv_f_w"][i].T.reshape(88, 128, 3).transpose(1, 0, 2))
    put("cfb", _col(inp["conv_f_b"][i], 88))
    put("l1g", _col(inp["ln1_g"][i], 16))
    put("l1b", _col(inp["ln1_b"][i], 16))
    put("l2g", _col(inp["ln2_g"][i], 16))
    put("l2b", _col(inp["ln2_b"][i], 16))
    put("ling", _col(inp["ln_in_g"], 16))
    put("linb", _col(inp["ln_in_b"], 16))
    put("nbg", _col(inp["norm_b_g"][i], 32))
    par2 = np.zeros((64, NP2), np.float32)
    par2[:, P2["dtb"]:P2["dtb"] + 64] = inp["dt_bias"][i][None, :]
    par2[:, P2["alog"]:P2["alog"] + 64] = inp["a_log"][i][None, :]
    par2[:, P2["dsk"]:P2["dsk"] + 64] = inp["d_skip"][i][None, :]
    return wblk, np.ascontiguousarray(wdt), par, par2


def prep_x(x_b, meta):
    hc = np.concatenate([meta, x_b], axis=0)
    hT = hc.T.reshape(16, 128, LT).transpose(1, 0, 2)
    xq = np.zeros((4, 128, 16, TP), np.float32)
    xq[0, :, :, 48:TP] = hT[:, :, 0:528]
    for q in range(1, 4):
        g0 = NMETA + 512 * q
        xq[q, :, :, 64:TP] = hT[:, :, g0:g0 + 512]
    return xq.reshape(4, 128, 16 * TP)


_NC_CACHE = {}


def kernel(**inputs):
    inp = {k: np.asarray(v, dtype=np.float32) for k, v in inputs.items()}
    if "nc" not in _NC_CACHE:
        _NC_CACHE["nc"] = build_program()
    nc = _NC_CACHE["nc"]
    lay = [prep_layer(i, inp) for i in range(L)]
    wblk = np.stack([a[0] for a in lay])
    wdt = np.stack([a[1] for a in lay])
    par = np.stack([a[2] for a in lay])
    par2 = np.stack([a[3] for a in lay])
    in_maps = []
    for c in range(8):
        b = c % 4
        in_maps.append({"xq": prep_x(inp["x"][b], inp["meta_tokens"]), "par": par, "par2": par2,
                        "wblk": wblk, "wdt": wdt})
    res = run_bass_kernel_spmd(nc, in_maps, core_ids=list(range(8)))
    out = np.zeros((4, SEQ, D), np.float32)
    for b in range(4):
        yq = np.asarray(res.results[b]["yq"]).reshape(4, 128, 16, 512)
        out[b] = yq.transpose(0, 3, 2, 1).reshape(SEQ, D)
    return out
```
